# Optimizing a Trainium2 kernel written in Bass

```python
import math
import jax, jax.numpy as jnp
from jax import lax
import numpy as np

D_MODEL = 1024
BATCH = 8
SEQ = 4096
DEPTH = 4

GRID_W = 64
CTX_LEN = 256
D_FOURIER = D_MODEL // 2
FOURIER_GROUPS = 4
FOURIER_GROUP_DIM = D_FOURIER // FOURIER_GROUPS
D_SSM = D_MODEL // 2
SSM_GROUP_DIM = 16
SSM_GROUPS = D_SSM // SSM_GROUP_DIM
SSM_STATE = 64
D_FF = ((8 * D_MODEL // 3 + 127) // 128) * 128
N_MOD = 9
D_PROJ = D_FOURIER + D_SSM + 2 * D_MODEL
RMS_EPS = 1e-6
DT_MIN = 1e-3
DT_MAX = 1e-1

kernel_name = "fnet_s5_macaron_gated_hybrid_dit"


def rms_norm(h, g):
    h32 = h.astype(jnp.float32)
    y = h32 * lax.rsqrt(jnp.mean(h32 * h32, axis=-1, keepdims=True) + RMS_EPS)
    return (y * g.astype(jnp.float32)).astype(h.dtype)


def modulated_norm(h, g, shift, scale):
    return rms_norm(h, g) * (1 + scale) + shift


def swiglu(h, w13, w2):
    a, b = jnp.split(h @ w13, 2, axis=-1)
    return (jax.nn.silu(a) * b) @ w2


def half_ffn(h, g, shift, scale, gate, w13, w2):
    return h + 0.5 * gate * swiglu(modulated_norm(h, g, shift, scale), w13, w2)


def fourier_mix(u):
    bsz, t, _ = u.shape
    ug = u.astype(jnp.float32).reshape(bsz, t, FOURIER_GROUPS, FOURIER_GROUP_DIM)
    yg = jnp.fft.fft2(ug, axes=(1, 3), norm="ortho").real
    return yg.reshape(bsz, t, D_FOURIER).astype(u.dtype)


def complex_mul(ar, ai, br, bi):
    return ar * br - ai * bi, ar * bi + ai * br


def ssm_discretise(lam_re, lam_im, log_step, b_re, b_im):
    f32 = jnp.float32
    lam_re = lam_re.astype(f32)
    lam_im = lam_im.astype(f32)
    dt = jnp.exp(log_step.astype(f32))[:, None]
    mag = jnp.exp(lam_re * dt)
    lb_re = mag * jnp.cos(lam_im * dt)
    lb_im = mag * jnp.sin(lam_im * dt)
    den = lam_re * lam_re + lam_im * lam_im
    nr = lb_re - 1.0
    ni = lb_im
    k_re = (nr * lam_re + ni * lam_im) / den
    k_im = (ni * lam_re - nr * lam_im) / den
    b_re = b_re.astype(f32)
    b_im = b_im.astype(f32)
    bb_re = k_re[..., None] * b_re - k_im[..., None] * b_im
    bb_im = k_re[..., None] * b_im + k_im[..., None] * b_re
    return lb_re, lb_im, bb_re, bb_im


def _scan_combine(e1, e2):
    a1r, a1i, b1r, b1i = e1
    a2r, a2i, b2r, b2i = e2
    ar, ai = complex_mul(a1r, a1i, a2r, a2i)
    br, bi = complex_mul(a2r, a2i, b1r, b1i)
    return ar, ai, br + b2r, bi + b2i


def ssm_scan(u, disc, h0, reverse):
    lb_re, lb_im, bb_re, bb_im = disc
    bu_re = jnp.einsum("btgh,gph->btgp", u, bb_re)
    bu_im = jnp.einsum("btgh,gph->btgp", u, bb_im)
    if h0 is not None:
        edge = -1 if reverse else 0
        ir, ii = complex_mul(lb_re, lb_im, h0[0], h0[1])
        bu_re = bu_re.at[:, edge].add(ir)
        bu_im = bu_im.at[:, edge].add(ii)
    t = u.shape[1]
    a_re = jnp.broadcast_to(lb_re, (1, t, SSM_GROUPS, SSM_STATE))
    a_im = jnp.broadcast_to(lb_im, (1, t, SSM_GROUPS, SSM_STATE))
    _, _, s_re, s_im = lax.associative_scan(
        _scan_combine, (a_re, a_im, bu_re, bu_im), reverse=reverse, axis=1)
    return s_re, s_im


def ssm_readout(s_re, s_im, c_re, c_im):
    c_re = c_re.astype(jnp.float32)
    c_im = c_im.astype(jnp.float32)
    return (jnp.einsum("btgp,ghp->btgh", s_re, c_re)
            - jnp.einsum("btgp,ghp->btgh", s_im, c_im))


def bidirectional_ssm(u_ctx, u_lat, lam_re, lam_im, log_step, b_re, b_im, c_re, c_im, d,
                      with_ctx_output):
    bsz = u_lat.shape[0]
    uc = u_ctx.astype(jnp.float32).reshape(bsz, -1, SSM_GROUPS, SSM_GROUP_DIM)
    ul = u_lat.astype(jnp.float32).reshape(bsz, -1, SSM_GROUPS, SSM_GROUP_DIM)
    dg = d.astype(jnp.float32).reshape(SSM_GROUPS, SSM_GROUP_DIM)
    y_lat = dg * ul
    y_ctx = dg * uc if with_ctx_output else None
    for direction, reverse in ((0, False), (1, True)):
        disc = ssm_discretise(lam_re[direction], lam_im[direction], log_step[direction],
                              b_re[direction], b_im[direction])
        sc_re, sc_im = ssm_scan(uc, disc, None, reverse)
        edge = 0 if reverse else -1
        h0 = (sc_re[:, edge], sc_im[:, edge])
        sl_re, sl_im = ssm_scan(ul, disc, h0, reverse)
        y_lat = y_lat + ssm_readout(sl_re, sl_im, c_re[direction], c_im[direction])
        if with_ctx_output:
            y_ctx = y_ctx + ssm_readout(sc_re, sc_im, c_re[direction], c_im[direction])
    y_lat = y_lat.reshape(bsz, -1, D_SSM).astype(u_lat.dtype)
    if with_ctx_output:
        y_ctx = y_ctx.reshape(bsz, -1, D_SSM).astype(u_ctx.dtype)
    return y_ctx, y_lat


def merge_branches(proj, y_ssm, w_fourier_out, w_glu, w_ssm_out, w_out):
    u_f = proj[..., :D_FOURIER]
    g_f = proj[..., D_FOURIER + D_SSM:D_FOURIER + D_SSM + D_MODEL]
    g_s = proj[..., D_FOURIER + D_SSM + D_MODEL:]
    f_branch = fourier_mix(u_f) @ w_fourier_out
    z = jax.nn.gelu(y_ssm)
    za, zg = jnp.split(z @ w_glu, 2, axis=-1)
    s_branch = (za * jax.nn.sigmoid(zg)) @ w_ssm_out
    return (jax.nn.sigmoid(g_f) * f_branch + jax.nn.sigmoid(g_s) * s_branch) @ w_out


def setup_inputs(seed: int = 0) -> dict:
    key = jax.random.key(seed)
    ks = jax.random.split(key, 25)
    f32 = jnp.float32

    def nrm(k, shape, fan_in):
        return jax.random.normal(k, shape, f32) * (fan_in ** -0.5)

    G, P, H = SSM_GROUPS, SSM_STATE, SSM_GROUP_DIM
    x = jax.random.normal(ks[0], (BATCH, SEQ, D_MODEL), f32)
    c = jax.random.normal(ks[1], (BATCH, D_MODEL), f32)
    ctx = jax.random.normal(ks[2], (BATCH, CTX_LEN, D_MODEL), f32)
    c_ctx = jax.random.normal(ks[3], (D_MODEL,), f32)
    w_mod = nrm(ks[4], (DEPTH, D_MODEL, N_MOD * D_MODEL), D_MODEL)
    b_mod = 0.02 * jax.random.normal(ks[5], (DEPTH, N_MOD * D_MODEL), f32)
    norm_g = 1.0 + 0.02 * jax.random.normal(ks[6], (DEPTH, 3, D_MODEL), f32)
    ffn1_w13 = nrm(ks[7], (DEPTH, D_MODEL, 2 * D_FF), D_MODEL)
    ffn1_w2 = nrm(ks[8], (DEPTH, D_FF, D_MODEL), D_FF)
    w_in = nrm(ks[9], (DEPTH, D_MODEL, D_PROJ), D_MODEL)
    w_fourier_out = nrm(ks[10], (DEPTH, D_FOURIER, D_MODEL), D_FOURIER)
    ssm_lambda_re = -0.5 + 0.01 * jax.random.normal(ks[11], (DEPTH, 2, G, P), f32)
    ssm_lambda_im = (math.pi * jnp.arange(P, dtype=f32)
                     + 0.01 * jax.random.normal(ks[12], (DEPTH, 2, G, P), f32))
    ssm_log_step = jax.random.uniform(ks[13], (DEPTH, 2, G), f32,
                                      minval=math.log(DT_MIN), maxval=math.log(DT_MAX))
    ssm_b_re = nrm(ks[14], (DEPTH, 2, G, P, H), 2 * H)
    ssm_b_im = nrm(ks[15], (DEPTH, 2, G, P, H), 2 * H)
    ssm_c_re = nrm(ks[16], (DEPTH, 2, G, H, P), P)
    ssm_c_im = nrm(ks[17], (DEPTH, 2, G, H, P), P)
    ssm_d = jax.random.normal(ks[18], (DEPTH, D_SSM), f32)
    w_glu = nrm(ks[19], (DEPTH, D_SSM, 2 * D_SSM), D_SSM)
    w_ssm_out = nrm(ks[20], (DEPTH, D_SSM, D_MODEL), D_SSM)
    w_out = nrm(ks[21], (DEPTH, D_MODEL, D_MODEL), D_MODEL)
    ffn2_w13 = nrm(ks[22], (DEPTH, D_MODEL, 2 * D_FF), D_MODEL)
    ffn2_w2 = nrm(ks[23], (DEPTH, D_FF, D_MODEL), D_FF)
    final_g = 1.0 + 0.02 * jax.random.normal(ks[24], (D_MODEL,), f32)
    return {"x": x, "c": c, "ctx": ctx, "c_ctx": c_ctx, "w_mod": w_mod, "b_mod": b_mod,
            "norm_g": norm_g, "ffn1_w13": ffn1_w13, "ffn1_w2": ffn1_w2, "w_in": w_in,
            "w_fourier_out": w_fourier_out, "ssm_lambda_re": ssm_lambda_re,
            "ssm_lambda_im": ssm_lambda_im, "ssm_log_step": ssm_log_step,
            "ssm_b_re": ssm_b_re, "ssm_b_im": ssm_b_im, "ssm_c_re": ssm_c_re,
            "ssm_c_im": ssm_c_im, "ssm_d": ssm_d, "w_glu": w_glu, "w_ssm_out": w_ssm_out,
            "w_out": w_out, "ffn2_w13": ffn2_w13, "ffn2_w2": ffn2_w2, "final_g": final_g}


def reference(x, c, ctx, c_ctx, w_mod, b_mod, norm_g, ffn1_w13, ffn1_w2, w_in,
              w_fourier_out, ssm_lambda_re, ssm_lambda_im, ssm_log_step, ssm_b_re,
              ssm_b_im, ssm_c_re, ssm_c_im, ssm_d, w_glu, w_ssm_out, w_out,
              ffn2_w13, ffn2_w2, final_g):
    h_lat = x
    h_ctx = ctx
    silu_c = jax.nn.silu(c)
    silu_cc = jax.nn.silu(c_ctx)
    s5_lo, s5_hi = D_FOURIER, D_FOURIER + D_SSM
    for layer in range(DEPTH):
        last = layer == DEPTH - 1
        m_lat = jnp.split((silu_c @ w_mod[layer] + b_mod[layer])[:, None, :], N_MOD, axis=-1)
        m_ctx = jnp.split(silu_cc @ w_mod[layer] + b_mod[layer], N_MOD, axis=-1)
        g = norm_g[layer]

        h_lat = half_ffn(h_lat, g[0], m_lat[0], m_lat[1], m_lat[2],
                         ffn1_w13[layer], ffn1_w2[layer])
        h_ctx = half_ffn(h_ctx, g[0], m_ctx[0], m_ctx[1], m_ctx[2],
                         ffn1_w13[layer], ffn1_w2[layer])

        p_lat = modulated_norm(h_lat, g[1], m_lat[3], m_lat[4]) @ w_in[layer]
        p_ctx = modulated_norm(h_ctx, g[1], m_ctx[3], m_ctx[4]) @ w_in[layer]
        y_ctx, y_lat = bidirectional_ssm(
            p_ctx[..., s5_lo:s5_hi], p_lat[..., s5_lo:s5_hi],
            ssm_lambda_re[layer], ssm_lambda_im[layer], ssm_log_step[layer],
            ssm_b_re[layer], ssm_b_im[layer], ssm_c_re[layer], ssm_c_im[layer],
            ssm_d[layer], not last)
        h_lat = h_lat + m_lat[5] * merge_branches(p_lat, y_lat, w_fourier_out[layer],
                                                  w_glu[layer], w_ssm_out[layer], w_out[layer])

        h_lat = half_ffn(h_lat, g[2], m_lat[6], m_lat[7], m_lat[8],
                         ffn2_w13[layer], ffn2_w2[layer])
        if not last:
            h_ctx = h_ctx + m_ctx[5] * merge_branches(p_ctx, y_ctx, w_fourier_out[layer],
                                                      w_glu[layer], w_ssm_out[layer], w_out[layer])
            h_ctx = half_ffn(h_ctx, g[2], m_ctx[6], m_ctx[7], m_ctx[8],
                             ffn2_w13[layer], ffn2_w2[layer])
    return rms_norm(h_lat, final_g)
```

```python
import math
from contextlib import ExitStack
import numpy as np
import ml_dtypes
import concourse.bass as bass
import concourse.mybir as mybir
from concourse.bass_utils import run_bass_kernel_spmd

F32 = mybir.dt.float32
BF16 = mybir.dt.bfloat16
I32 = mybir.dt.int32
AF = mybir.ActivationFunctionType
ALU = mybir.AluOpType

D = 1024
KC = 8
DFF = 2816
FC = 22
TL = 4096
TCX = 256
TT = TL + TCX
G = 32
H = 16
NCL = 512
NCC = 32
NCH = NCL + NCC
ARENA_WORDS = 52600
ENGS = ['pe', 'act', 'dve', 'pool', 'sp']


class Prog:
    def __init__(self, nc):
        self.nc = nc
        self.q = {e: [] for e in ENGS}
        self.cnt = {}
        self.waited = {}
        self.lastw = {}
        self.readers = {}
        self.semnames = []

    def _sem(self, name):
        if name not in self.cnt:
            self.cnt[name] = 0
            self.semnames.append(name)

    def _wait(self, eng, tok):
        s, v = tok
        if self.waited.get((eng, s), 0) < v:
            self.q[eng].append(('w', s, v))
            self.waited[(eng, s)] = v

    def op(self, eng, fn, reads=(), writes=(), key=None, nosame=False):
        own = 'E_' + eng
        toks = []
        for r in reads:
            if r in self.lastw:
                toks.append(self.lastw[r])
        for w in writes:
            if w in self.lastw:
                toks.append(self.lastw[w])
            toks.extend(self.readers.get(w, ()))
        for t in toks:
            if eng == 'pe' and t[0] == 'E_pe':
                continue
            if nosame and t[0] == own:
                continue
            self._wait(eng, t)
        if key is None:
            self._sem(own)
            self.cnt[own] += 1
            tok = (own, self.cnt[own])
            self.q[eng].append(('o', fn, own, 1))
        else:
            self._sem(key)
            self.cnt[key] += 16
            tok = (key, self.cnt[key])
            self.q[eng].append(('o', fn, key, 16))
        for r in reads:
            self.readers.setdefault(r, []).append(tok)
        for w in writes:
            self.lastw[w] = tok
            self.readers[w] = []
        return tok

    def barrier(self):
        for e in ENGS:
            for s in self.semnames:
                if self.cnt[s] > 0:
                    self._wait(e, (s, self.cnt[s]))
        self.lastw = {}
        self.readers = {}

    def emit(self):
        nc = self.nc
        with ExitStack() as es:
            sems = {n: es.enter_context(nc.semaphore(n)) for n in self.semnames}
            block = es.enter_context(nc.Block())

            def mk(engname):
                def f(e):
                    for it in self.q[engname]:
                        if it[0] == 'w':
                            e.wait_ge(sems[it[1]], it[2])
                        else:
                            it[1](e).then_inc(sems[it[2]], it[3])
                return f
            block.tensor(mk('pe'))
            block.scalar(mk('act'))
            block.vector(mk('dve'))
            block.gpsimd(mk('pool'))
            block.sync(mk('sp'))


class Mem:
    def __init__(self, arena, nwords):
        self.a = arena
        self.n = nwords
        self.top = 0

    def f32(self, n, shape=None):
        nr = (n + 7) // 8 * 8
        off = self.top
        self.top += nr
        assert self.top <= self.n, ("SBUF overflow", self.top, self.n)
        return self.a[:, off:off + n]

    def bf(self, n):
        nw = ((n + 1) // 2 + 7) // 8 * 8
        off = self.top
        self.top += nw
        assert self.top <= self.n, ("SBUF overflow", self.top, self.n)
        return self.a[:, off:off + nw].bitcast(BF16)[:, 0:n]


def build(depth, stop=None):
    nc = bass.Bass("TRN2", target_bir_lowering=False)
    P = Prog(nc)
    L = depth

    def din(name, shape, dt=F32):
        return nc.dram_tensor(name, list(shape), dt, kind="ExternalInput").ap()

    xT = din("xT", [D, TT])
    scin = din("scin", [128, 16])
    wmod = din("w_mod", [L, D, 9 * D])
    bmodT = din("bmodT", [128, L * 72])
    ngT = din("ngT", [128, L * 24])
    fgT = din("fgT", [128, 8])
    w13d = [din("ffn1_w13", [L, D, 2 * DFF]), din("ffn2_w13", [L, D, 2 * DFF])]
    w2d = [din("ffn1_w2", [L, DFF, D]), din("ffn2_w2", [L, DFF, D])]
    wind = din("w_in", [L, D, 3 * D])
    wfod = din("w_fourier_out", [L, 512, D])
    wglud = din("w_glu", [L, 512, D])
    wsod = din("w_ssm_out", [L, 512, D])
    woutd = din("w_out", [L, D, D])
    ssml = din("ssm_small", [128, L * 2 * 3 * G])
    ssmbc = din("ssm_bc", [128, L * 2 * 4 * G * H])
    dmd = din("dm", [128, L * G])
    constd = din("consts", [128, 3 * 128 + 34])
    cscd = din("csc", [128, 256], BF16)
    tabd = din("tab", [8, 128, 32 * 2 * 512], BF16)
    tabcd = din("tabc", [128, 2 * 2 * 256], BF16)
    yT = nc.dram_tensor("yT", [D, TL], F32, kind="ExternalOutput").ap()
    hS = nc.dram_tensor("hS", [D, TT], F32).ap()
    upD = nc.dram_tensor("upD", [D, TT], BF16).ap()
    zD = nc.dram_tensor("zD", [512, TT], BF16).ap()
    ytD = nc.dram_tensor("ytD", [512, TT], BF16).ap()
    c_w13 = [nc.dram_tensor("c_w13_%d" % i, [D, 2 * DFF], BF16).ap() for i in range(2)]
    c_w2 = [nc.dram_tensor("c_w2_%d" % i, [DFF, D], BF16).ap() for i in range(2)]
    c_wg = nc.dram_tensor("c_wg", [D, 2 * D], BF16).ap()
    c_wfo = nc.dram_tensor("c_wfo", [512, D], BF16).ap()
    c_wglu = nc.dram_tensor("c_wglu", [512, D], BF16).ap()
    c_wso = nc.dram_tensor("c_wso", [512, D], BF16).ap()
    c_wout = nc.dram_tensor("c_wout", [D, D], BF16).ap()

    es = ExitStack()
    arena = es.enter_context(nc.sbuf_tensor("arena", [128, ARENA_WORDS], F32))
    psall = es.enter_context(nc.psum_tensor("psall", [128, 8 * 512], F32))
    M = Mem(arena, ARENA_WORDS)

    def bank(i, n=512):
        return psall[:, i * 512:i * 512 + n]

    mods = M.f32(L * 144)
    ngs = M.f32(L * 24)
    fgs = M.f32(8)
    cst = M.f32(3 * 128 + 34)
    ident = cst[:, 0:128]
    maskf = cst[:, 128:256]
    maskb = cst[:, 256:384]
    nvec = cst[:, 384:418]
    ssc = M.f32(16)
    epsT = M.f32(8)
    onesb = M.bf(128)
    persist_top = M.top

    def mod_ap(l, m, typ):
        v = mods[:, l * 144:(l + 1) * 144].rearrange("p (c t) -> p c t", t=2)
        return v[:, m * 8:(m + 1) * 8, typ]

    P.op('sp', lambda e: e.dma_start(out=ngs, in_=ngT), writes=['ngs'], key='p0')
    P.op('sp', lambda e: e.dma_start(out=fgs, in_=fgT), writes=['fgs'], key='p1')
    P.op('sp', lambda e: e.dma_start(out=cst, in_=constd), writes=['cst'], key='p2')
    P.op('sp', lambda e: e.dma_start(out=ssc, in_=scin), writes=['ssc'], key='p3')
    P.op('dve', lambda e: e.memset(onesb, 1.0), writes=['ones'])
    P.op('dve', lambda e: e.memset(epsT[:, 0:1], 1e-6), writes=['eps'])
    P.op('dve', lambda e: e.memset(epsT[:, 1:2], -math.pi), writes=['eps'])
    P.op('dve', lambda e: e.memset(epsT[:, 2:3], 0.0), writes=['eps'])
    P.op('act', lambda e: e.activation(out=ssc, in_=ssc, func=AF.Silu), reads=['ssc'], writes=['ssc'])
    eps_ap = epsT[:, 0:1]
    negpi_ap = epsT[:, 1:2]

    wm = [M.f32(8 * 512), M.f32(8 * 512)]
    bmt = M.f32(L * 72)
    P.op('sp', lambda e: e.dma_start(out=bmt, in_=bmodT), writes=['bmt'], key='p4')
    sscv = ssc.rearrange("p (k t) -> p k t", t=2)
    for l in range(L):
        psm = bank(l % 2, 144).rearrange("p (c t) -> p c t", t=2)
        wsrc = wmod[l].rearrange("(k p) n -> p k n", p=128)
        for blk in range(18):
            buf = wm[blk % 2]
            bufv = buf.rearrange("p (k n) -> p k n", k=8)
            P.op('sp', lambda e, bufv=bufv, wsrc=wsrc, blk=blk: e.dma_start(out=bufv, in_=wsrc[:, :, blk * 512:(blk + 1) * 512]),
                 writes=[('wm', blk % 2)], key='wm%d' % (blk % 2))
            for cc in range(4):
                ch = blk * 4 + cc
                for kc in range(8):
                    P.op('pe', lambda e, psm=psm, bufv=bufv, cc=cc, kc=kc, ch=ch: e.matmul(
                        out=psm[:, ch, :], lhsT=bufv[:, kc, cc * 128:(cc + 1) * 128], rhs=sscv[:, kc, :],
                        start=(kc == 0), stop=(kc == 7)),
                        reads=[('wm', blk % 2), 'ssc'], writes=[('psm', l % 2)])
        mv = mods[:, l * 144:(l + 1) * 144].rearrange("p (c t) -> p c t", t=2)
        for typ in range(2):
            P.op('dve', lambda e, mv=mv, psm=psm, typ=typ, l=l: e.tensor_tensor(
                out=mv[:, :, typ], in0=psm[:, :, typ], in1=bmt[:, l * 72:(l + 1) * 72], op=ALU.add),
                reads=[('psm', l % 2), 'bmt'], writes=['mods'])

    tiles512 = [(i * 512, 512, 0) for i in range(8)] + [(TL, 256, 1)]
    tiles256 = [(i * 256, 256, 0) for i in range(16)] + [(TL, 256, 1)]

    def sublayer_scalars(l, s, gate_scale):
        A = M.f32(16).rearrange("p (k t) -> p k t", t=2)
        SH = M.f32(16).rearrange("p (k t) -> p k t", t=2)
        GT = M.f32(16).rearrange("p (k t) -> p k t", t=2)
        gv = ngs[:, l * 24 + s * 8: l * 24 + s * 8 + 8]
        for typ in range(2):
            P.op('dve', lambda e, typ=typ: e.scalar_tensor_tensor(
                out=A[:, :, typ], in0=mod_ap(l, 3 * s + 1, typ), scalar=1.0, in1=gv, op0=ALU.add, op1=ALU.mult),
                reads=['mods', 'ngs'], writes=['A'])
            P.op('dve', lambda e, typ=typ: e.tensor_copy(out=SH[:, :, typ], in_=mod_ap(l, 3 * s, typ)),
                 reads=['mods'], writes=['SH'])
            P.op('dve', lambda e, typ=typ: e.tensor_scalar(
                out=GT[:, :, typ], in0=mod_ap(l, 3 * s + 2, typ), scalar1=float(gate_scale), scalar2=None, op0=ALU.mult),
                reads=['mods'], writes=['GT'])
        return A, SH, GT

    class NormBufs:
        def __init__(self, W, nx=1):
            self.W = W
            self.nx = nx
            self.hb = [M.f32(8 * W).rearrange("p (k w) -> p k w", k=8) for _ in range(2)]
            self.xns = [M.bf(8 * W).rearrange("p (k w) -> p k w", k=8) for _ in range(nx)]
            self.xn = self.xns[0]
            self.hsq = [M.bf(W) for _ in range(2)]
            self.rstd = M.f32(W)
            self.sq = M.f32(W)
            self.tmp = [M.f32(W) for _ in range(2)]

    def load_h(nb, i, src, t0, W):
        hb = nb.hb[i % 2]
        srcv = src.rearrange("(k p) t -> p k t", p=128)
        P.op('sp', lambda e: e.dma_start(out=hb[:, :, 0:W], in_=srcv[:, :, t0:t0 + W]),
             writes=[('hb', i % 2)], key='hb%d' % (i % 2))

    def rstd_of(nb, i, W, psbank):
        hb = nb.hb[i % 2]
        ssq = bank(psbank, W)
        for kc in range(8):
            hq = nb.hsq[kc % 2]
            P.op('dve', lambda e, hq=hq, kc=kc: e.tensor_tensor(out=hq[:, 0:W], in0=hb[:, kc, 0:W], in1=hb[:, kc, 0:W], op=ALU.mult),
                 reads=[('hb', i % 2)], writes=[('hsq', kc % 2)])
            P.op('pe', lambda e, hq=hq, kc=kc: e.matmul(out=ssq, lhsT=onesb, rhs=hq[:, 0:W], start=(kc == 0), stop=(kc == 7)),
                 reads=[('hsq', kc % 2), 'ones'], writes=[('ps', psbank)])
        P.op('act', lambda e: e.activation(out=nb.sq[:, 0:W], in_=ssq, func=AF.Sqrt, bias=eps_ap, scale=1.0 / D),
             reads=[('ps', psbank), 'eps'], writes=['sq'])
        P.op('dve', lambda e: e.reciprocal(out=nb.rstd[:, 0:W], in_=nb.sq[:, 0:W]), reads=['sq'], writes=['rstd'])

    def norm_tile(nb, i, W, typ, A, SH, psbank):
        hb = nb.hb[i % 2]
        xn_ = nb.xns[i % nb.nx]
        xres = ('xn', i % nb.nx)
        rstd_of(nb, i, W, psbank)
        for kc in range(8):
            tm = nb.tmp[kc % 2]
            P.op('dve', lambda e, tm=tm, kc=kc: e.tensor_tensor(out=tm[:, 0:W], in0=hb[:, kc, 0:W], in1=nb.rstd[:, 0:W], op=ALU.mult),
                 reads=[('hb', i % 2), 'rstd'], writes=[('tmp', kc % 2)])
            P.op('act', lambda e, tm=tm, kc=kc: e.activation(out=xn_[:, kc, 0:W], in_=tm[:, 0:W], func=AF.Identity,
                                                              bias=SH[:, kc, typ:typ + 1], scale=A[:, kc, typ:typ + 1]),
                 reads=[('tmp', kc % 2), 'A', 'SH'], writes=[xres])

    def load_w(dst3, src2, nk, key, res, conv=False):
        srcv = src2.rearrange("(k p) n -> p k n", p=128)
        if conv:
            step = 4 if nk >= 8 else nk
            for k in range(0, nk, step):
                k1 = min(nk, k + step)
                P.op('sp', lambda e, k=k, k1=k1: e.dma_start(out=dst3[:, k:k1, :], in_=srcv[:, k:k1, :]), writes=[res], key=key)
            return
        for k in range(nk):
            P.op('pool', lambda e, k=k: e.dma_start(out=dst3[:, k, :], in_=srcv[:, k, :], max_dma_last_dim=4096),
                 writes=[res], key=key)

    def convert_w(dst2, src2, key, nsplit=4):
        rows = src2.shape[0]
        rs = rows // nsplit
        for q in range(nsplit):
            P.op('pool', lambda e, q=q: e.dma_start(out=dst2[q * rs:(q + 1) * rs, :], in_=src2[q * rs:(q + 1) * rs, :], max_dma_last_dim=4096),
                 key=key)

    def ffn_phase(l, which, src, dst):
        P.barrier()
        M.top = persist_top
        s = 0 if which == 0 else 2
        w13 = M.bf(8 * 2 * DFF).rearrange("p (k f) -> p k f", k=8)
        w2 = M.bf(FC * D).rearrange("p (k d) -> p k d", k=FC)
        use_conv = not (l == 0 and which == 0)
        if use_conv:
            load_w(w13, c_w13[which], 8, 'w13', 'w13', conv=True)
            load_w(w2, c_w2[which], FC, 'w2', 'w2', conv=True)
        else:
            load_w(w13, w13d[which][l], 8, 'w13', 'w13')
            load_w(w2, w2d[which][l], FC, 'w2', 'w2')

        def issue_conversions():
            if which == 0:
                convert_w(c_w13[1], w13d[1][l], 'cv0')
                convert_w(c_w2[1], w2d[1][l], 'cv1')
                convert_w(c_wg, wind[l][:, 1024:3072], 'cv2')
                convert_w(c_wfo, wfod[l], 'cv3', 2)
                convert_w(c_wglu, wglud[l], 'cv4', 2)
                convert_w(c_wso, wsod[l], 'cv5', 2)
                convert_w(c_wout, woutd[l], 'cv6', 2)
            elif l + 1 < L:
                convert_w(c_w13[0], w13d[0][l + 1], 'cv0')
                convert_w(c_w2[0], w2d[0][l + 1], 'cv1')
        W = 256
        nb = NormBufs(W)
        gbuf = M.bf(FC * W).rearrange("p (k w) -> p k w", k=FC)
        sil = [M.f32(W) for _ in range(2)]
        A, SH, GT = sublayer_scalars(l, s, 0.5)
        tiles = tiles256
        n = len(tiles)

        def phaseA(i):
            for fc in range(FC):
                pa = bank(1 + (fc % 2) * 2, W)
                pb = bank(2 + (fc % 2) * 2, W)
                for kc in range(8):
                    P.op('pe', lambda e, pa=pa, fc=fc, kc=kc: e.matmul(out=pa, lhsT=w13[:, kc, fc * 128:(fc + 1) * 128], rhs=nb.xn[:, kc, 0:W],
                                                                          start=(kc == 0), stop=(kc == 7)),
                         reads=['w13', ('xn', 0)], writes=[('ps', 1 + (fc % 2) * 2)])
                for kc in range(8):
                    P.op('pe', lambda e, pb=pb, fc=fc, kc=kc: e.matmul(out=pb, lhsT=w13[:, kc, DFF + fc * 128:DFF + (fc + 1) * 128], rhs=nb.xn[:, kc, 0:W],
                                                                          start=(kc == 0), stop=(kc == 7)),
                         reads=['w13', ('xn', 0)], writes=[('ps', 2 + (fc % 2) * 2)])
                sl = sil[fc % 2]
                P.op('act', lambda e, pa=pa, sl=sl: e.activation(out=sl, in_=pa, func=AF.Silu),
                     reads=[('ps', 1 + (fc % 2) * 2)], writes=[('sil', fc % 2)])
                P.op('dve', lambda e, pb=pb, sl=sl, fc=fc: e.tensor_tensor(out=gbuf[:, fc, :], in0=sl, in1=pb, op=ALU.mult),
                     reads=[('sil', fc % 2), ('ps', 2 + (fc % 2) * 2)], writes=[('g', fc)])

        def phaseB(i):
            t0, _, typ = tiles[i]
            hb = nb.hb[i % 2]
            for dc in range(8):
                pb_i = 5 + dc % 3
                po = bank(pb_i, W)
                for fc in range(FC):
                    P.op('pe', lambda e, po=po, fc=fc, dc=dc: e.matmul(out=po, lhsT=w2[:, fc, dc * 128:(dc + 1) * 128], rhs=gbuf[:, fc, :],
                                                                          start=(fc == 0), stop=(fc == FC - 1)),
                         reads=['w2', ('g', fc)], writes=[('ps', pb_i)])
                P.op('dve', lambda e, po=po, dc=dc, typ=typ: e.scalar_tensor_tensor(
                    out=hb[:, dc, :], in0=po, scalar=GT[:, dc, typ:typ + 1], in1=hb[:, dc, :], op0=ALU.mult, op1=ALU.add),
                    reads=[('ps', pb_i), 'GT', ('hb', i % 2)], writes=[('hb', i % 2)])
            dstv = dst.rearrange("(k p) t -> p k t", p=128)
            P.op('sp', lambda e: e.dma_start(out=dstv[:, :, t0:t0 + W], in_=hb), reads=[('hb', i % 2)], key='st%d' % (i % 2))

        load_h(nb, 0, src, tiles[0][0], W)
        norm_tile(nb, 0, W, tiles[0][2], A, SH, 0)
        for i in range(n):
            if i + 1 < n:
                load_h(nb, i + 1, src, tiles[i + 1][0], W)
            phaseA(i)
            if i + 1 < n:
                norm_tile(nb, i + 1, W, tiles[i + 1][2], A, SH, 0)
            phaseB(i)
            if i == 1:
                issue_conversions()

    def m1_phase(l):
        P.barrier()
        M.top = persist_top
        win = M.bf(8 * 1024).rearrange("p (k n) -> p k n", k=8)
        srcv = wind[l].rearrange("(k p) n -> p k n", p=128)
        for k in range(8):
            P.op('pool', lambda e, k=k: e.dma_start(out=win[:, k, :], in_=srcv[:, k, 0:1024], max_dma_last_dim=4096),
                 writes=['win'], key='win')
        W = 512
        nb = NormBufs(W, nx=2)
        stg = [M.bf(8 * W).rearrange("p (k w) -> p k w", k=8) for _ in range(2)]
        A, SH, GT = sublayer_scalars(l, 1, 1.0)
        tiles = tiles512
        n = len(tiles)
        upv = upD.rearrange("(k p) t -> p k t", p=128)
        load_h(nb, 0, hS, tiles[0][0], tiles[0][1])
        load_h(nb, 1, hS, tiles[1][0], tiles[1][1])
        norm_tile(nb, 0, tiles[0][1], tiles[0][2], A, SH, 0)
        for i in range(n):
            t0, Wt, typ = tiles[i]
            if i + 1 < n:
                norm_tile(nb, i + 1, tiles[i + 1][1], tiles[i + 1][2], A, SH, 0)
            if i + 2 < n:
                load_h(nb, i + 2, hS, tiles[i + 2][0], tiles[i + 2][1])
            st = stg[i % 2]
            xn_ = nb.xns[i % 2]
            for oc in range(8):
                pb_i = 1 + oc % 4
                po = bank(pb_i, Wt)
                for kc in range(8):
                    P.op('pe', lambda e, po=po, oc=oc, kc=kc, Wt=Wt, xn_=xn_: e.matmul(out=po, lhsT=win[:, kc, oc * 128:(oc + 1) * 128], rhs=xn_[:, kc, 0:Wt],
                                                                                 start=(kc == 0), stop=(kc == 7)),
                         reads=['win', ('xn', i % 2)], writes=[('ps', pb_i)])
                if oc % 2 == 0:
                    P.op('act', lambda e, po=po, oc=oc, Wt=Wt, st=st: e.copy(out=st[:, oc, 0:Wt], in_=po),
                         reads=[('ps', pb_i)], writes=[('stg', i % 2)])
                else:
                    P.op('dve', lambda e, po=po, oc=oc, Wt=Wt, st=st: e.tensor_copy(out=st[:, oc, 0:Wt], in_=po),
                         reads=[('ps', pb_i)], writes=[('stg', i % 2)])
            P.op('sp', lambda e, st=st, t0=t0, Wt=Wt: e.dma_start(out=upv[:, :, t0:t0 + Wt], in_=st[:, :, 0:Wt]),
                 reads=[('stg', i % 2)], writes=['upD'], key='stu%d' % (i % 2))

    def ssm_phase(l):
        P.barrier()
        M.top = persist_top
        BM = [[M.bf(G * 128).rearrange("p (g n) -> p g n", g=G) for _ in range(2)] for _ in range(2)]
        RM = [[M.bf(G * 128).rearrange("p (g n) -> p g n", g=G) for _ in range(2)] for _ in range(2)]
        MMb = M.bf(G * 128).rearrange("p (g n) -> p g n", g=G)
        LAMre = [M.f32(16) for _ in range(2)]
        LAMim = [M.f32(16) for _ in range(2)]
        LAMimN = [M.f32(16) for _ in range(2)]
        LAMimS = [M.f32(32).rearrange("p (r g) -> p r g", r=2) for _ in range(2)]
        dms = M.f32(G)
        main_top = M.top
        small = M.f32(2 * 3 * G).rearrange("p (d k g) -> p d k g", d=2, k=3)
        bc = M.f32(2 * 4 * G * H).rearrange("p (d k g h) -> p d k g h", d=2, k=4, g=G)
        P.op('sp', lambda e: e.dma_start(out=small, in_=ssml[:, l * 192:(l + 1) * 192].rearrange("p (d k g) -> p d k g", d=2, k=3)),
             writes=['small'], key='p0')
        P.op('sp', lambda e: e.dma_start(out=bc, in_=ssmbc[:, l * 4096:(l + 1) * 4096].rearrange("p (d k g h) -> p d k g h", d=2, k=4, g=G)),
             writes=['bc'], key='p1')
        P.op('sp', lambda e: e.dma_start(out=dms, in_=dmd[:, l * G:(l + 1) * G]), writes=['dms'], key='p2')
        NV = 34
        dt_ = M.f32(G)
        a_ = M.f32(G)
        th_ = M.f32(G)
        den = M.f32(G)
        t32a = M.f32(G)
        t32b = M.f32(G)
        kre = M.f32(G)
        kim = M.f32(G)
        PWre = M.f32(G * NV).rearrange("p (g n) -> p g n", g=G)
        PWim = M.f32(G * NV).rearrange("p (g n) -> p g n", g=G)
        bbre = M.f32(G * H).rearrange("p (g h) -> p g h", g=G)
        bbim = M.f32(G * H).rearrange("p (g h) -> p g h", g=G)
        t512a = M.f32(G * H).rearrange("p (g h) -> p g h", g=G)
        t512b = M.f32(G * H).rearrange("p (g h) -> p g h", g=G)
        NE = G * 8 * H

        def bigraw():
            return M.f32(NE)

        def v4(r):
            return r.rearrange("p (g j h) -> p g j h", g=G, j=8)

        def tabv(r, k):
            return r[:, k * G * NV:(k + 1) * G * NV].rearrange("p (g n) -> p g n", g=G)
        PRr, PIr, T1r, RZr, MMr = bigraw(), bigraw(), bigraw(), bigraw(), bigraw()
        PR, PI, T1, RZ, MMacc = v4(PRr), v4(PIr), v4(T1r), v4(RZr), v4(MMr)
        AN, YS, FR = tabv(T1r, 0), tabv(T1r, 1), tabv(T1r, 2)
        YI = RZr[:, 0:G * NV].bitcast(I32).rearrange("p (g n) -> p g n", g=G)
        MK, MG = tabv(RZr, 1), tabv(RZr, 2)
        SI, CO = tabv(PIr, 0), tabv(PIr, 1)
        mtmp = M.f32(128)
        for d in range(2):
            for ri in range(2):
                P.op('pool', lambda e, d=d, ri=ri: e.memset(BM[d][ri], 0.0), writes=[('BM', d)])
                P.op('pool', lambda e, d=d, ri=ri: e.memset(RM[d][ri], 0.0), writes=[('RM', d)])

        def V(fn, reads, writes, eng='dve'):
            P.op(eng, fn, reads=reads, writes=writes)

        nvb = nvec.unsqueeze(1).to_broadcast([128, G, NV])

        def cprod(slc, Xre, Xim, rd):
            pre = PWre[:, :, slc].unsqueeze(3).to_broadcast([128, G, 8, H])
            pim = PWim[:, :, slc].unsqueeze(3).to_broadcast([128, G, 8, H])
            xre = Xre.unsqueeze(2).to_broadcast([128, G, 8, H])
            xim = Xim.unsqueeze(2).to_broadcast([128, G, 8, H])
            V(lambda e: e.tensor_tensor(out=PR, in0=pre, in1=xre, op=ALU.mult), ['PW'] + rd, ['PR'])
            V(lambda e: e.tensor_tensor(out=T1, in0=pim, in1=xim, op=ALU.mult), ['PW'] + rd, ['T1'])
            V(lambda e: e.tensor_tensor(out=PR, in0=PR, in1=T1, op=ALU.subtract), ['PR', 'T1'], ['PR'])
            V(lambda e: e.tensor_tensor(out=PI, in0=pre, in1=xim, op=ALU.mult), ['PW'] + rd, ['PI'])
            V(lambda e: e.tensor_tensor(out=T1, in0=pim, in1=xre, op=ALU.mult), ['PW', 'PR'] + rd, ['T1'])
            V(lambda e: e.tensor_tensor(out=PI, in0=PI, in1=T1, op=ALU.add), ['PI', 'T1'], ['PI'])

        for d in range(2):
            lamre = small[:, d, 0, :]
            lamim = small[:, d, 1, :]
            lstep = small[:, d, 2, :]
            V(lambda e, lstep=lstep: e.activation(out=dt_, in_=lstep, func=AF.Exp), ['small'], ['dt'], 'act')
            V(lambda e, lamre=lamre: e.tensor_tensor(out=a_, in0=lamre, in1=dt_, op=ALU.mult), ['small', 'dt'], ['a'])
            V(lambda e, lamim=lamim: e.tensor_tensor(out=th_, in0=lamim, in1=dt_, op=ALU.mult), ['small', 'dt'], ['th'])
            V(lambda e, lamre=lamre: e.tensor_tensor(out=den, in0=lamre, in1=lamre, op=ALU.mult), ['small'], ['den'])
            V(lambda e, lamim=lamim: e.tensor_tensor(out=t32a, in0=lamim, in1=lamim, op=ALU.mult), ['small'], ['t32a'])
            V(lambda e: e.tensor_tensor(out=den, in0=den, in1=t32a, op=ALU.add), ['den', 't32a'], ['den'])
            V(lambda e: e.reciprocal(out=den, in_=den), ['den'], ['den'])
            P.barrier()
            thb = th_.unsqueeze(2).to_broadcast([128, G, NV])
            ab = a_.unsqueeze(2).to_broadcast([128, G, NV])
            V(lambda e, thb=thb: e.tensor_tensor(out=AN, in0=thb, in1=nvb, op=ALU.mult), ['th', 'cst'], ['AN'])
            V(lambda e, ab=ab: e.tensor_tensor(out=MG, in0=ab, in1=nvb, op=ALU.mult), ['a', 'cst'], ['MG'])
            V(lambda e: e.activation(out=MG, in_=MG, func=AF.Exp), ['MG'], ['MG'], 'act')
            for (dst_t, off) in ((SI, 64.5), (CO, 64.75)):
                V(lambda e, off=off: e.tensor_scalar(out=YS, in0=AN, scalar1=1.0 / (2 * math.pi), scalar2=off, op0=ALU.mult, op1=ALU.add),
                  ['AN', 'FR'], ['YS'])
                V(lambda e: e.tensor_copy(out=YI, in_=YS), ['YS'], ['YI'])
                V(lambda e: e.tensor_copy(out=FR, in_=YI), ['YI'], ['FR'])
                V(lambda e: e.tensor_tensor(out=FR, in0=YS, in1=FR, op=ALU.subtract), ['YS', 'FR'], ['FR'])
                V(lambda e: e.tensor_scalar(out=MK, in0=FR, scalar1=0.0, scalar2=None, op0=ALU.is_lt), ['FR'], ['MK'])
                V(lambda e: e.tensor_tensor(out=FR, in0=FR, in1=MK, op=ALU.add), ['FR', 'MK'], ['FR'])
                V(lambda e, dst_t=dst_t: e.activation(out=dst_t, in_=FR, func=AF.Sin, bias=negpi_ap, scale=2 * math.pi),
                  ['FR', 'eps'], ['SICO'], 'act')
            V(lambda e: e.tensor_tensor(out=PWre, in0=MG, in1=CO, op=ALU.mult), ['MG', 'SICO'], ['PW'])
            V(lambda e: e.tensor_tensor(out=PWim, in0=MG, in1=SI, op=ALU.mult), ['MG', 'SICO'], ['PW'])
            P.barrier()
            lbre = PWre[:, :, 9]
            lbim = PWim[:, :, 9]
            V(lambda e, lbre=lbre: e.tensor_scalar(out=t32a, in0=lbre, scalar1=-1.0, scalar2=None, op0=ALU.add), ['PW'], ['t32a'])
            V(lambda e, lamre=lamre: e.tensor_tensor(out=kre, in0=t32a, in1=lamre, op=ALU.mult), ['t32a', 'small'], ['kre'])
            V(lambda e, lbim=lbim, lamim=lamim: e.tensor_tensor(out=t32b, in0=lbim, in1=lamim, op=ALU.mult), ['PW', 'small'], ['t32b'])
            V(lambda e: e.tensor_tensor(out=kre, in0=kre, in1=t32b, op=ALU.add), ['kre', 't32b'], ['kre'])
            V(lambda e: e.tensor_tensor(out=kre, in0=kre, in1=den, op=ALU.mult), ['kre', 'den'], ['kre'])
            V(lambda e, lbim=lbim, lamre=lamre: e.tensor_tensor(out=kim, in0=lbim, in1=lamre, op=ALU.mult), ['PW', 'small'], ['kim'])
            V(lambda e, lamim=lamim: e.tensor_tensor(out=t32b, in0=t32a, in1=lamim, op=ALU.mult), ['t32a', 'small'], ['t32b'])
            V(lambda e: e.tensor_tensor(out=kim, in0=kim, in1=t32b, op=ALU.subtract), ['kim', 't32b'], ['kim'])
            V(lambda e: e.tensor_tensor(out=kim, in0=kim, in1=den, op=ALU.mult), ['kim', 'den'], ['kim'])
            bre = bc[:, d, 0, :, :]
            bim = bc[:, d, 1, :, :]
            cre = bc[:, d, 2, :, :]
            cim = bc[:, d, 3, :, :]
            kreb = kre.unsqueeze(2).to_broadcast([128, G, H])
            kimb = kim.unsqueeze(2).to_broadcast([128, G, H])
            V(lambda e, bre=bre, kreb=kreb: e.tensor_tensor(out=t512a, in0=bre, in1=kreb, op=ALU.mult), ['bc', 'kre'], ['t512a'])
            V(lambda e, bim=bim, kimb=kimb: e.tensor_tensor(out=t512b, in0=bim, in1=kimb, op=ALU.mult), ['bc', 'kim'], ['t512b'])
            V(lambda e: e.tensor_tensor(out=bbre, in0=t512a, in1=t512b, op=ALU.subtract), ['t512a', 't512b'], ['bb'])
            V(lambda e, bim=bim, kreb=kreb: e.tensor_tensor(out=t512a, in0=bim, in1=kreb, op=ALU.mult), ['bc', 'kre', 'bb'], ['t512a'])
            V(lambda e, bre=bre, kimb=kimb: e.tensor_tensor(out=t512b, in0=bre, in1=kimb, op=ALU.mult), ['bc', 'kim', 'bb'], ['t512b'])
            V(lambda e: e.tensor_tensor(out=bbim, in0=t512a, in1=t512b, op=ALU.add), ['t512a', 't512b'], ['bb'])
            for (dstl, srcw, sgn) in ((LAMre[d], PWre, 1.0), (LAMim[d], PWim, 1.0), (LAMimN[d], PWim, -1.0)):
                sv = srcw[:, :, 16].rearrange("p (gp g2) -> p gp g2", g2=2)
                V(lambda e, dstl=dstl, sv=sv, sgn=sgn: e.tensor_scalar(out=dstl[0:64, :], in0=sv[0:64, :, 0], scalar1=sgn, scalar2=None, op0=ALU.mult),
                  ['PW'], [('LAM', d)])
                V(lambda e, dstl=dstl, sv=sv, sgn=sgn: e.tensor_scalar(out=dstl[64:128, :], in0=sv[64:128, :, 1], scalar1=sgn, scalar2=None, op0=ALU.mult),
                  ['PW'], [('LAM', d)])
            V(lambda e, d=d: e.tensor_copy(out=LAMimS[d][:, 0, :], in_=LAMimN[d]), [('LAM', d)], [('LAM', d)])
            V(lambda e, d=d: e.tensor_copy(out=LAMimS[d][:, 1, :], in_=LAMim[d]), [('LAM', d)], [('LAM', d)])
            if d == 0:
                sB, sR, sZ = slice(17 + 1, 17 + 9), slice(9, 17), slice(17 + 9, 17 + 17)
            else:
                sB, sR, sZ = slice(8, 16), slice(17 + 0, 17 + 8), slice(0, 8)
            cprod(sB, bbre, bbim, ['bb'])
            V(lambda e: e.tensor_copy(out=PR[64:128], in_=PI[64:128]), ['PI', 'PR'], ['PR'])
            for g in range(G):
                pt = bank(g % 2, 128)
                P.op('pe', lambda e, pt=pt, g=g: e.transpose(out=pt, in_=PR[:, g, :, :].rearrange("p j h -> p (j h)"), identity=ident),
                     reads=['PR', 'cst'], writes=[('ps', g % 2)])
                c0 = (g % 2) * 64
                P.op('act', lambda e, pt=pt, g=g, c0=c0, d=d: e.copy(out=BM[d][0][:, g, c0:c0 + 64], in_=pt[:, 0:64]),
                     reads=[('ps', g % 2)], writes=[('BM', d)])
                P.op('dve', lambda e, pt=pt, g=g, c0=c0, d=d: e.tensor_copy(out=BM[d][1][:, g, c0:c0 + 64], in_=pt[:, 64:128]),
                     reads=[('ps', g % 2)], writes=[('BM', d)])
            cprod(sR, cre, cim, ['bc'])
            prv = PR.rearrange("p (gp g2) j h -> p gp g2 (j h)", g2=2)
            piv = PI.rearrange("p (gp g2) j h -> p gp g2 (j h)", g2=2)
            rre = RM[d][0].rearrange("p (gp g2) n -> p gp g2 n", g2=2)
            rim = RM[d][1].rearrange("p (gp g2) n -> p gp g2 n", g2=2)
            for hf in range(2):
                ps_ = slice(hf * 64, hf * 64 + 64)
                V(lambda e, ps_=ps_, hf=hf, rre=rre: e.tensor_copy(out=rre[ps_, :, hf, :], in_=prv[ps_, :, hf, :]), ['PR'], [('RM', d)])
                V(lambda e, ps_=ps_, hf=hf, rim=rim: e.tensor_scalar(out=rim[ps_, :, hf, :], in0=piv[ps_, :, hf, :], scalar1=-1.0, scalar2=None, op0=ALU.mult),
                  ['PI'], [('RM', d)])
            V(lambda e: e.tensor_copy(out=RZ[0:64], in_=PR[0:64]), ['PR', ('psM', 0), ('psM', 1)], ['RZ'])
            V(lambda e: e.tensor_scalar(out=RZ[64:128], in0=PI[64:128], scalar1=-1.0, scalar2=None, op0=ALU.mult), ['PI'], ['RZ'])
            cprod(sZ, bbre, bbim, ['bb'])
            V(lambda e: e.tensor_copy(out=PR[64:128], in_=PI[64:128]), ['PI', 'PR'], ['PR'])
            mask = maskf if d == 0 else maskb
            for g in range(G):
                pm = bank(2 + g % 2, 128)
                P.op('pe', lambda e, pm=pm, g=g: e.matmul(out=pm, lhsT=PR[:, g, :, :].rearrange("p j h -> p (j h)"),
                                                            rhs=RZ[:, g, :, :].rearrange("p j h -> p (j h)"), start=True, stop=True),
                     reads=['PR', 'RZ'], writes=[('psM', g % 2)])
                mg = MMacc[:, g, :, :].rearrange("p j h -> p (j h)")
                if d == 0:
                    V(lambda e, pm=pm, mg=mg, mask=mask: e.tensor_tensor(out=mg, in0=pm, in1=mask, op=ALU.mult), [('psM', g % 2), 'cst'], ['MMacc'])
                else:
                    V(lambda e, pm=pm, mask=mask: e.tensor_tensor(out=mtmp, in0=pm, in1=mask, op=ALU.mult), [('psM', g % 2), 'cst'], ['mtmp'])
                    V(lambda e, mg=mg: e.tensor_tensor(out=mg, in0=mg, in1=mtmp, op=ALU.add), ['mtmp', 'MMacc'], ['MMacc'])
                    V(lambda e, mg=mg, g=g: e.scalar_tensor_tensor(out=MMb[:, g, :], in0=ident, scalar=dms[:, g:g + 1], in1=mg, op0=ALU.mult, op1=ALU.add),
                      ['MMacc', 'dms', 'cst'], ['MMb'])
        P.barrier()
        M.top = main_top
        U = M.bf(G * NCH).rearrange("p (g c) -> p g c", g=G)
        SP = [[M.bf(16 * NCH).rearrange("p (g c) -> p g c", g=16) for _ in range(2)] for _ in range(2)]
        WB = [[M.f32(2 * 16 * 32).rearrange("p (r g c) -> p r g c", r=2, g=16) for _ in range(2)] for _ in range(2)]
        TS1 = [M.f32(32).rearrange("p (r g) -> p r g", r=2) for _ in range(2)]
        TS2 = [M.f32(32).rearrange("p (r g) -> p r g", r=2) for _ in range(2)]
        GA = [M.f32(NCH) for _ in range(2)]
        GB = [M.f32(NCH) for _ in range(2)]
        for j in range(8):
            P.op('sp', lambda e, j=j: e.dma_start(out=U[16 * j:16 * j + 16, :, 0:NCL],
                                                  in_=upD[512:1024, j * NCL:(j + 1) * NCL].rearrange("(g h) c -> h g c", h=H)),
                 reads=['upD'], writes=[('U', g_) for g_ in range(G)], key='ldU')
            P.op('sp', lambda e, j=j: e.dma_start(out=U[16 * j:16 * j + 16, :, NCL:NCH],
                                                  in_=upD[512:1024, TL + j * NCC:TL + (j + 1) * NCC].rearrange("(g h) c -> h g c", h=H)),
                 reads=['upD'], writes=[('U', g_) for g_ in range(G)], key='ldU')
        P.op('pool', lambda e: e.memset(SP[0][0][:, :, NCL:NCL + 1], 0.0), writes=[('SP', 0)])
        P.op('pool', lambda e: e.memset(SP[0][1][:, :, NCL:NCL + 1], 0.0), writes=[('SP', 0)])
        P.op('pool', lambda e: e.memset(SP[1][0][:, :, NCH - 1:NCH], 0.0), writes=[('SP', 1)])
        P.op('pool', lambda e: e.memset(SP[1][1][:, :, NCH - 1:NCH], 0.0), writes=[('SP', 1)])

        blocks = {0: [(NCL, True)] + [(b * 32, False) for b in range(16)],
                  1: [(NCL, True)] + [(b * 32, False) for b in range(15, -1, -1)]}
        prev_ap = {0: None, 1: None}
        lre_b = [LAMre[d].unsqueeze(1).to_broadcast([128, 2, 16]) for d in range(2)]
        for k in range(17):
            wbs = {}
            for d in range(2):
                c0, isctx = blocks[d][k]
                wb = WB[d][k % 2]
                wbs[d] = wb
                for ri in range(2):
                    pw = bank(d * 2 + ri, 512).rearrange("p (g c) -> p g c", g=16)
                    for gp in range(16):
                        for g2 in range(2):
                            g = 2 * gp + g2
                            P.op('pe', lambda e, pw=pw, gp=gp, g=g, g2=g2, d=d, ri=ri, c0=c0: e.matmul(
                                out=pw[:, gp, :], lhsT=BM[d][ri][:, g, :], rhs=U[:, g, c0:c0 + 32], start=(g2 == 0), stop=(g2 == 1)),
                                reads=[('BM', d), ('U', g)], writes=[('psW', d, ri)])
                    P.op('act', lambda e, pw=pw, wb=wb, ri=ri: e.copy(out=wb[:, ri, :, :], in_=pw),
                         reads=[('psW', d, ri)], writes=[('WB', d, k % 2)])
            for st_ in range(32):
                ops = {0: [], 1: []}
                for d in range(2):
                    cc = st_ if d == 0 else 31 - st_
                    wb = wbs[d]
                    cur = wb[:, :, :, cc]
                    pv = prev_ap[d]
                    res = ('WB', d, k % 2)
                    if pv is not None:
                        pvap, pvres = pv
                        t1 = TS1[d]
                        t2 = TS2[d]
                        ops[d].append((lambda e, t1=t1, pvap=pvap, d=d: e.tensor_tensor(out=t1, in0=pvap, in1=lre_b[d], op=ALU.mult),
                                       [pvres, ('LAM', d)], [('TS1', d)]))
                        ops[d].append((lambda e, t2=t2, pvap=pvap, d=d: e.tensor_tensor(out=t2, in0=pvap[:, ::-1, :], in1=LAMimS[d], op=ALU.mult),
                                       [pvres, ('LAM', d)], [('TS2', d)]))
                        ops[d].append((lambda e, cur=cur, t1=t1: e.tensor_tensor(out=cur, in0=cur, in1=t1, op=ALU.add),
                                       [res, ('TS1', d)], [res]))
                        ops[d].append((lambda e, cur=cur, t2=t2: e.tensor_tensor(out=cur, in0=cur, in1=t2, op=ALU.add),
                                       [res, ('TS2', d)], [res]))
                    prev_ap[d] = (cur, res)
                n0, n1 = len(ops[0]), len(ops[1])
                for j in range(max(n0, n1)):
                    for d in range(2):
                        if j < len(ops[d]):
                            fn, rd, wr = ops[d][j]
                            P.op('dve', fn, reads=rd, writes=wr, nosame=(n0 == n1 and n0 > 0))
            for d in range(2):
                c0, isctx = blocks[d][k]
                wb = wbs[d]
                for ri in range(2):
                    sp_ = SP[d][ri]
                    if d == 0:
                        if isctx:
                            P.op('act', lambda e, sp_=sp_, wb=wb, ri=ri: e.copy(out=sp_[:, :, NCL + 1:NCH], in_=wb[:, ri, :, 0:31]),
                                 reads=[('WB', d, k % 2)], writes=[('SP', d)])
                            P.op('act', lambda e, sp_=sp_, wb=wb, ri=ri: e.copy(out=sp_[:, :, 0:1], in_=wb[:, ri, :, 31:32]),
                                 reads=[('WB', d, k % 2)], writes=[('SP', d)])
                        else:
                            nv_ = 32 if c0 + 33 <= NCL else 31
                            P.op('act', lambda e, sp_=sp_, wb=wb, ri=ri, c0=c0, nv_=nv_: e.copy(out=sp_[:, :, c0 + 1:c0 + 1 + nv_], in_=wb[:, ri, :, 0:nv_]),
                                 reads=[('WB', d, k % 2)], writes=[('SP', d)])
                    else:
                        if isctx:
                            P.op('act', lambda e, sp_=sp_, wb=wb, ri=ri: e.copy(out=sp_[:, :, NCL:NCH - 1], in_=wb[:, ri, :, 1:32]),
                                 reads=[('WB', d, k % 2)], writes=[('SP', d)])
                            P.op('act', lambda e, sp_=sp_, wb=wb, ri=ri: e.copy(out=sp_[:, :, NCL - 1:NCL], in_=wb[:, ri, :, 0:1]),
                                 reads=[('WB', d, k % 2)], writes=[('SP', d)])
                        else:
                            if c0 == 0:
                                P.op('act', lambda e, sp_=sp_, wb=wb, ri=ri: e.copy(out=sp_[:, :, 0:31], in_=wb[:, ri, :, 1:32]),
                                     reads=[('WB', d, k % 2)], writes=[('SP', d)])
                            else:
                                P.op('act', lambda e, sp_=sp_, wb=wb, ri=ri, c0=c0: e.copy(out=sp_[:, :, c0 - 1:c0 + 31], in_=wb[:, ri, :, 0:32]),
                                     reads=[('WB', d, k % 2)], writes=[('SP', d)])
        C1 = 1.5957691216057308
        for g in range(G):
            gp = g // 2
            b0 = 4 + (g % 2) * 2
            py = psall[:, b0 * 512:b0 * 512 + NCH]
            for (lo, hi) in ((0, NCL), (NCL, NCH)):
                out_ap = psall[:, b0 * 512 + lo:b0 * 512 + hi]
                P.op('pe', lambda e, out_ap=out_ap, g=g, lo=lo, hi=hi: e.matmul(out=out_ap, lhsT=MMb[:, g, :], rhs=U[:, g, lo:hi], start=True, stop=False),
                     reads=['MMb', ('U', g)], writes=[('psY', g % 2)])
                idx = 0
                for d in range(2):
                    for ri in range(2):
                        idx += 1
                        P.op('pe', lambda e, out_ap=out_ap, g=g, gp=gp, lo=lo, hi=hi, d=d, ri=ri, idx=idx: e.matmul(
                            out=out_ap, lhsT=RM[d][ri][:, g, :], rhs=SP[d][ri][:, gp, lo:hi], start=False, stop=(idx == 4)),
                            reads=[('RM', d), ('SP', d)], writes=[('psY', g % 2)])
            ga = GA[g % 2]
            gb = GB[g % 2]
            P.op('act', lambda e, ga=ga, py=py: e.activation(out=ga, in_=py, func=AF.Square), reads=[('psY', g % 2)], writes=[('GA', g % 2)])
            P.op('dve', lambda e, ga=ga, gb=gb: e.tensor_scalar(out=gb, in0=ga, scalar1=0.044715, scalar2=1.0, op0=ALU.mult, op1=ALU.add),
                 reads=[('GA', g % 2)], writes=[('GB', g % 2)])
            P.op('dve', lambda e, gb=gb, py=py: e.tensor_tensor(out=gb, in0=gb, in1=py, op=ALU.mult),
                 reads=[('GB', g % 2), ('psY', g % 2)], writes=[('GB', g % 2)])
            P.op('act', lambda e, ga=ga, gb=gb: e.activation(out=ga, in_=gb, func=AF.Sigmoid, scale=C1), reads=[('GB', g % 2)], writes=[('GA', g % 2)])
            P.op('dve', lambda e, ga=ga, py=py, g=g: e.tensor_tensor(out=U[:, g, :], in0=ga, in1=py, op=ALU.mult),
                 reads=[('GA', g % 2), ('psY', g % 2)], writes=[('U', g)])
        for j in range(8):
            P.op('sp', lambda e, j=j: e.dma_start(out=zD[:, j * NCL:(j + 1) * NCL].rearrange("(g h) c -> h g c", h=H),
                                                  in_=U[16 * j:16 * j + 16, :, 0:NCL]),
                 reads=[('U', g_) for g_ in range(G)], writes=['zD'], key='stz')
            P.op('sp', lambda e, j=j: e.dma_start(out=zD[:, TL + j * NCC:TL + (j + 1) * NCC].rearrange("(g h) c -> h g c", h=H),
                                                  in_=U[16 * j:16 * j + 16, :, NCL:NCH]),
                 reads=[('U', g_) for g_ in range(G)], writes=['zD'], key='stz')

    def fm_phase(l, dst):
        P.barrier()
        M.top = persist_top
        YT = M.bf(4 * TT).rearrange("p (g t) -> p g t", g=4)
        yt_top = M.top
        AB = M.bf(34 * 4 * 256).rearrange("p (c g n) -> p c g n", c=34, g=4)
        csc = M.bf(256)
        P.op('sp', lambda e: e.dma_start(out=csc, in_=cscd), writes=['csc'], key='p0')
        uf = [M.bf(4 * 512).rearrange("p (g t) -> p g t", g=4) for _ in range(2)]
        tb = [M.bf(4 * 2 * 512).rearrange("p (c s n) -> p c s n", c=4, s=2) for _ in range(2)]
        tbc = M.bf(2 * 2 * 256).rearrange("p (c s n) -> p c s n", c=2, s=2)
        upv = upD[0:512, :].rearrange("(g p) t -> p g t", p=128)
        for i, (t0, Wt, typ) in enumerate(tiles512):
            u = uf[i % 2]
            P.op('sp', lambda e, u=u, t0=t0, Wt=Wt: e.dma_start(out=u[:, :, 0:Wt], in_=upv[:, :, t0:t0 + Wt]),
                 reads=['upD'], writes=[('uf', i % 2)], key='uf%d' % (i % 2))
            for sub in range(Wt // 128):
                tc = t0 // 128 + sub
                for gh in range(2):
                    pb_i = (sub * 2 + gh) % 8
                    pa = bank(pb_i).rearrange("p (g n) -> p g n", g=2)
                    for g2 in range(2):
                        g = gh * 2 + g2
                        P.op('pe', lambda e, pa=pa, g2=g2, g=g, u=u, sub=sub: e.matmul(out=pa[:, g2, :], lhsT=u[:, g, sub * 128:(sub + 1) * 128], rhs=csc,
                                                                                        start=True, stop=True),
                             reads=[('uf', i % 2), 'csc'], writes=[('ps', pb_i)])
                    eng = 'act' if gh == 0 else 'dve'
                    if eng == 'act':
                        P.op('act', lambda e, pa=pa, tc=tc, gh=gh: e.copy(out=AB[:, tc, gh * 2:gh * 2 + 2, :], in_=pa), reads=[('ps', pb_i)], writes=['AB'])
                    else:
                        P.op('dve', lambda e, pa=pa, tc=tc, gh=gh: e.tensor_copy(out=AB[:, tc, gh * 2:gh * 2 + 2, :], in_=pa), reads=[('ps', pb_i)], writes=['AB'])
        cnt = 0
        for tt in range(8):
            for tcg in range(8):
                tbuf = tb[cnt % 2]
                P.op('sp', lambda e, tbuf=tbuf, tt=tt, tcg=tcg: e.dma_start(
                    out=tbuf, in_=tabd[tt, :, tcg * 4096:(tcg + 1) * 4096].rearrange("p (c s n) -> p c s n", c=4, s=2)),
                    writes=[('tb', cnt % 2)], key='tb%d' % (cnt % 2))
                for c4 in range(4):
                    tc = tcg * 4 + c4
                    for cs in range(2):
                        for g in range(4):
                            pb_i = (tt % 2) * 4 + g
                            first = (tcg == 0 and c4 == 0 and cs == 0)
                            last = (tcg == 7 and c4 == 3 and cs == 1)
                            P.op('pe', lambda e, pb_i=pb_i, tc=tc, g=g, cs=cs, tbuf=tbuf, c4=c4, first=first, last=last: e.matmul(
                                out=bank(pb_i), lhsT=AB[:, tc, g, cs * 128:(cs + 1) * 128], rhs=tbuf[:, c4, cs, :], start=first, stop=last),
                                reads=['AB', ('tb', cnt % 2)], writes=[('ps', pb_i)])
                cnt += 1
            for g in range(4):
                pb_i = (tt % 2) * 4 + g
                if g % 2 == 0:
                    P.op('act', lambda e, pb_i=pb_i, g=g, tt=tt: e.copy(out=YT[:, g, tt * 512:(tt + 1) * 512], in_=bank(pb_i)), reads=[('ps', pb_i)], writes=['YT'])
                else:
                    P.op('dve', lambda e, pb_i=pb_i, g=g, tt=tt: e.tensor_copy(out=YT[:, g, tt * 512:(tt + 1) * 512], in_=bank(pb_i)), reads=[('ps', pb_i)], writes=['YT'])
        P.op('sp', lambda e: e.dma_start(out=tbc, in_=tabcd.rearrange("p (c s n) -> p c s n", c=2, s=2)), writes=['tbc'], key='p1')
        for g in range(4):
            k = 0
            for c2 in range(2):
                for cs in range(2):
                    P.op('pe', lambda e, g=g, c2=c2, cs=cs, k=k: e.matmul(out=bank(g, 256), lhsT=AB[:, 32 + c2, g, cs * 128:(cs + 1) * 128], rhs=tbc[:, c2, cs, :],
                                                                          start=(k == 0), stop=(k == 3)),
                         reads=['AB', 'tbc'], writes=[('ps', g)])
                    k += 1
            P.op('act', lambda e, g=g: e.copy(out=YT[:, g, TL:TT], in_=bank(g, 256)), reads=[('ps', g)], writes=['YT'])

        P.op('sp', lambda e: e.dma_start(out=ytD.rearrange("(g p) t -> p g t", p=128), in_=YT), reads=['YT'], writes=['ytD'], key='p2')
        P.barrier()
        M.top = persist_top
        wg = M.bf(8 * 2048).rearrange("p (k n) -> p k n", k=8)
        load_w(wg, c_wg, 8, 'wg', 'wg', conv=True)
        wfo = M.bf(4 * D).rearrange("p (k n) -> p k n", k=4)
        wglu = M.bf(4 * D).rearrange("p (k n) -> p k n", k=4)
        wso = M.bf(4 * D).rearrange("p (k n) -> p k n", k=4)
        wout = M.bf(8 * D).rearrange("p (k n) -> p k n", k=8)
        load_w(wfo, c_wfo, 4, 'wfo', 'wfo', conv=True)
        load_w(wglu, c_wglu, 4, 'wglu', 'wglu', conv=True)
        load_w(wso, c_wso, 4, 'wso', 'wso', conv=True)
        load_w(wout, c_wout, 8, 'wout', 'wout', conv=True)
        W = 512
        nb = NormBufs(W, nx=2)
        zt = [M.bf(4 * W).rearrange("p (k w) -> p k w", k=4) for _ in range(2)]
        yb = [M.bf(4 * W).rearrange("p (k w) -> p k w", k=4) for _ in range(2)]
        ytv = ytD.rearrange("(g p) t -> p g t", p=128)
        glu = M.bf(4 * W).rearrange("p (k w) -> p k w", k=4)
        mb = M.bf(8 * W).rearrange("p (k w) -> p k w", k=8)
        sgb = M.bf(16 * W).rearrange("p (k w) -> p k w", k=16)
        sg = [M.f32(W) for _ in range(2)]
        e1 = [M.f32(W) for _ in range(2)]
        e2 = [M.f32(W) for _ in range(2)]
        A, SH, GT = sublayer_scalars(l, 1, 1.0)
        tiles = tiles512
        n = len(tiles)
        zv = zD.rearrange("(k p) t -> p k t", p=128)
        dstv = dst.rearrange("(k p) t -> p k t", p=128)
        load_h(nb, 0, hS, tiles[0][0], tiles[0][1])
        norm_tile(nb, 0, tiles[0][1], tiles[0][2], A, SH, 0)
        for i in range(n):
            t0, Wt, typ = tiles[i]
            hb = nb.hb[i % 2]
            xn_ = nb.xns[i % 2]
            xres = ('xn', i % 2)
            z = zt[i % 2]
            P.op('sp', lambda e, z=z, t0=t0, Wt=Wt: e.dma_start(out=z[:, :, 0:Wt], in_=zv[:, :, t0:t0 + Wt]), reads=['zD'], writes=[('zt', i % 2)], key='zt%d' % (i % 2))
            y_ = yb[i % 2]
            P.op('sp', lambda e, y_=y_, t0=t0, Wt=Wt: e.dma_start(out=y_[:, :, 0:Wt], in_=ytv[:, :, t0:t0 + Wt]), reads=['ytD'], writes=[('yb', i % 2)], key='yb%d' % (i % 2))
            for oc in range(16):
                pb_i = 1 + oc % 4
                for kc in range(8):
                    P.op('pe', lambda e, pb_i=pb_i, oc=oc, kc=kc, Wt=Wt, xn_=xn_: e.matmul(out=bank(pb_i, Wt), lhsT=wg[:, kc, oc * 128:(oc + 1) * 128], rhs=xn_[:, kc, 0:Wt],
                                                                                       start=(kc == 0), stop=(kc == 7)),
                         reads=['wg', xres], writes=[('ps', pb_i)])
                P.op('act', lambda e, pb_i=pb_i, oc=oc, Wt=Wt: e.activation(out=sgb[:, oc, 0:Wt], in_=bank(pb_i, Wt), func=AF.Sigmoid),
                     reads=[('ps', pb_i)], writes=[('sgb', oc)])
            if i + 1 < n:
                load_h(nb, i + 1, hS, tiles[i + 1][0], tiles[i + 1][1])
                norm_tile(nb, i + 1, tiles[i + 1][1], tiles[i + 1][2], A, SH, 0)
            for oc in range(4):
                pa_i, pg_i = 5 + (oc % 2) * 2 - (0 if oc % 2 == 0 else 0), 6 + (oc % 2) * 2 - (0 if oc % 2 == 0 else 8 * 0)
                if oc % 2 == 1:
                    pa_i, pg_i = 7, 1
                for (pb_i, off) in ((pa_i, 0), (pg_i, 512)):
                    for kc in range(4):
                        P.op('pe', lambda e, pb_i=pb_i, off=off, oc=oc, kc=kc, z=z, Wt=Wt: e.matmul(
                            out=bank(pb_i, Wt), lhsT=wglu[:, kc, off + oc * 128:off + (oc + 1) * 128], rhs=z[:, kc, 0:Wt], start=(kc == 0), stop=(kc == 3)),
                            reads=['wglu', ('zt', i % 2)], writes=[('ps', pb_i)])
                s_ = sg[oc % 2]
                P.op('act', lambda e, s_=s_, pg_i=pg_i, Wt=Wt: e.activation(out=s_[:, 0:Wt], in_=bank(pg_i, Wt), func=AF.Sigmoid), reads=[('ps', pg_i)], writes=[('sg', oc % 2)])
                P.op('dve', lambda e, s_=s_, pa_i=pa_i, oc=oc, Wt=Wt: e.tensor_tensor(out=glu[:, oc, 0:Wt], in0=s_[:, 0:Wt], in1=bank(pa_i, Wt), op=ALU.mult),
                     reads=[('sg', oc % 2), ('ps', pa_i)], writes=[('glu', oc)])
            sets = [(2, 3), (4, 5), (6, 7)]
            for dc in range(8):
                bf_i, bs_i = sets[dc % 3]
                par = dc % 2
                for kc in range(4):
                    P.op('pe', lambda e, bf_i=bf_i, dc=dc, kc=kc, Wt=Wt, y_=y_: e.matmul(out=bank(bf_i, Wt), lhsT=wfo[:, kc, dc * 128:(dc + 1) * 128], rhs=y_[:, kc, 0:Wt],
                                                                                       start=(kc == 0), stop=(kc == 3)),
                         reads=['wfo', ('yb', i % 2)], writes=[('ps', bf_i)])
                for kc in range(4):
                    P.op('pe', lambda e, bs_i=bs_i, dc=dc, kc=kc, Wt=Wt: e.matmul(out=bank(bs_i, Wt), lhsT=wso[:, kc, dc * 128:(dc + 1) * 128], rhs=glu[:, kc, 0:Wt],
                                                                                start=(kc == 0), stop=(kc == 3)),
                         reads=['wso', ('glu', kc)], writes=[('ps', bs_i)])
                x1, x2 = e1[par], e2[par]
                P.op('dve', lambda e, x1=x1, bf_i=bf_i, dc=dc, Wt=Wt: e.tensor_tensor(out=x1[:, 0:Wt], in0=sgb[:, dc, 0:Wt], in1=bank(bf_i, Wt), op=ALU.mult),
                     reads=[('sgb', dc), ('ps', bf_i)], writes=[('e1', par)])
                P.op('dve', lambda e, x2=x2, bs_i=bs_i, dc=dc, Wt=Wt: e.tensor_tensor(out=x2[:, 0:Wt], in0=sgb[:, 8 + dc, 0:Wt], in1=bank(bs_i, Wt), op=ALU.mult),
                     reads=[('sgb', 8 + dc), ('ps', bs_i)], writes=[('e2', par)])
                P.op('pool', lambda e, x1=x1, x2=x2, dc=dc, Wt=Wt: e.tensor_tensor(out=mb[:, dc, 0:Wt], in0=x1[:, 0:Wt], in1=x2[:, 0:Wt], op=ALU.add),
                     reads=[('e1', par), ('e2', par)], writes=[('mb', dc)])
            for dc in range(8):
                pb_i = 1 if dc % 2 == 0 else 2
                for kc in range(8):
                    P.op('pe', lambda e, pb_i=pb_i, dc=dc, kc=kc, Wt=Wt: e.matmul(out=bank(pb_i, Wt), lhsT=wout[:, kc, dc * 128:(dc + 1) * 128], rhs=mb[:, kc, 0:Wt],
                                                                                start=(kc == 0), stop=(kc == 7)),
                         reads=['wout', ('mb', kc)], writes=[('ps', pb_i)])
                P.op('dve', lambda e, pb_i=pb_i, dc=dc, typ=typ, Wt=Wt, hb=hb: e.scalar_tensor_tensor(
                    out=hb[:, dc, 0:Wt], in0=bank(pb_i, Wt), scalar=GT[:, dc, typ:typ + 1], in1=hb[:, dc, 0:Wt], op0=ALU.mult, op1=ALU.add),
                    reads=[('ps', pb_i), 'GT', ('hb', i % 2)], writes=[('hb', i % 2)])
            P.op('sp', lambda e, hb=hb, t0=t0, Wt=Wt: e.dma_start(out=dstv[:, :, t0:t0 + Wt], in_=hb[:, :, 0:Wt]), reads=[('hb', i % 2)], key='st%d' % (i % 2))

    def final_phase(src):
        P.barrier()
        M.top = persist_top
        W = 512
        nb = NormBufs(W)
        ob = [M.f32(8 * W).rearrange("p (k w) -> p k w", k=8) for _ in range(2)]
        yv = yT.rearrange("(k p) t -> p k t", p=128)
        tiles = tiles512[:8]
        load_h(nb, 0, src, 0, W)
        for i, (t0, Wt, typ) in enumerate(tiles):
            if i + 1 < len(tiles):
                load_h(nb, i + 1, src, tiles[i + 1][0], W)
            hb = nb.hb[i % 2]
            rstd_of(nb, i, W, 0)
            o = ob[i % 2]
            for kc in range(8):
                tm = nb.tmp[kc % 2]
                P.op('dve', lambda e, tm=tm, kc=kc, hb=hb: e.tensor_tensor(out=tm, in0=hb[:, kc, :], in1=nb.rstd, op=ALU.mult),
                     reads=[('hb', i % 2), 'rstd'], writes=[('tmp', kc % 2)])
                P.op('act', lambda e, tm=tm, kc=kc, o=o: e.activation(out=o[:, kc, :], in_=tm, func=AF.Identity, scale=fgs[:, kc:kc + 1]),
                     reads=[('tmp', kc % 2), 'fgs'], writes=[('ob', i % 2)])
            P.op('pool', lambda e, o=o, t0=t0: e.dma_start(out=yv[:, :, t0:t0 + W], in_=o), reads=[('ob', i % 2)], key='sty%d' % (i % 2))

    src = xT
    for l in range(L):
        ffn_phase(l, 0, src, hS)
        src = hS
        if stop == ('ffn1', l):
            break
        m1_phase(l)
        ssm_phase(l)
        fm_phase(l, hS)
        if stop == ('mix', l):
            break
        ffn_phase(l, 1, hS, hS)
    final_phase(hS)
    P.barrier()
    P.emit()
    es.close()
    return nc


def _perm_tokens(a, nchunk):
    T = a.shape[0]
    return a.reshape(nchunk, 8, -1).transpose(1, 0, 2).reshape(T, -1)


def _unperm_tokens(a, nchunk):
    T = a.shape[0]
    return a.reshape(8, nchunk, -1).transpose(1, 0, 2).reshape(T, -1)


_CONST_CACHE = {}


def _host_consts():
    if _CONST_CACHE:
        return _CONST_CACHE
    bf = ml_dtypes.bfloat16
    cst = np.zeros((128, 3 * 128 + 34), np.float32)
    cst[:, 0:128] = np.eye(128, dtype=np.float32)
    jj = np.arange(128) // 16
    cst[:, 128:256] = (jj[None, :] >= jj[:, None]).astype(np.float32)
    cst[:, 256:384] = (jj[:, None] >= jj[None, :]).astype(np.float32)
    nv = np.concatenate([np.arange(-8, 9), np.arange(8, -9, -1)]).astype(np.float32)
    cst[:, 384:418] = nv[None, :]
    c = np.arange(128)
    ang = 2 * np.pi * np.outer(c, c) / 128.0
    csc = np.concatenate([np.cos(ang), np.sin(ang)], axis=1) / np.sqrt(128.0)

    def seq_tab(T, nchunk):
        pos = np.arange(T)
        tok = 8 * (pos % nchunk) + pos // nchunk
        prod = (np.outer(tok, tok) % T).astype(np.float64)
        ang = 2 * np.pi * prod / T
        s = 1.0 / np.sqrt(T)
        return (np.cos(ang) * s).astype(np.float32), (-np.sin(ang) * s).astype(np.float32)
    Cl, Sl = seq_tab(TL, NCL)
    tab = np.stack([Cl, Sl], axis=0)
    tab = tab.reshape(2, 32, 128, 8, 512)
    tab = np.ascontiguousarray(tab.transpose(3, 2, 1, 0, 4)).reshape(8, 128, 32 * 2 * 512).astype(bf)
    Cc, Sc = seq_tab(TCX, NCC)
    tabc = np.stack([Cc, Sc], axis=0).reshape(2, 2, 128, 256)
    tabc = np.ascontiguousarray(tabc.transpose(2, 1, 0, 3)).reshape(128, 2 * 2 * 256).astype(bf)
    _CONST_CACHE.update(consts=cst, csc=csc.astype(bf), tab=tab, tabc=tabc)
    return _CONST_CACHE


def make_in_maps(inputs, depth, cores):
    f = lambda a: np.ascontiguousarray(np.asarray(a, dtype=np.float32))
    x = f(inputs["x"]); c = f(inputs["c"]); ctx = f(inputs["ctx"]); c_ctx = f(inputs["c_ctx"])
    L = depth
    consts = _host_consts()
    shared = dict(consts)
    shared["w_mod"] = f(inputs["w_mod"][:L])
    shared["bmodT"] = np.ascontiguousarray(f(inputs["b_mod"][:L]).reshape(L, 72, 128).transpose(2, 0, 1)).reshape(128, L * 72)
    shared["ngT"] = np.ascontiguousarray(f(inputs["norm_g"][:L]).reshape(L, 3, 8, 128).transpose(3, 0, 1, 2)).reshape(128, L * 24)
    shared["fgT"] = np.ascontiguousarray(f(inputs["final_g"]).reshape(8, 128).T)
    for k in ["ffn1_w13", "ffn1_w2", "ffn2_w13", "ffn2_w2", "w_in", "w_fourier_out", "w_glu", "w_ssm_out", "w_out"]:
        shared[k] = f(inputs[k][:L])
    lre = f(inputs["ssm_lambda_re"][:L]); lim = f(inputs["ssm_lambda_im"][:L]); ls = f(inputs["ssm_log_step"][:L])
    sm = np.stack([lre.transpose(3, 0, 1, 2), lim.transpose(3, 0, 1, 2),
                   np.broadcast_to(ls[None], (64, L, 2, G))], axis=3)
    shared["ssm_small"] = np.ascontiguousarray(np.tile(sm, (2, 1, 1, 1, 1))).reshape(128, L * 2 * 3 * G)
    bre = f(inputs["ssm_b_re"][:L]).transpose(3, 0, 1, 2, 4); bim = f(inputs["ssm_b_im"][:L]).transpose(3, 0, 1, 2, 4)
    cre = f(inputs["ssm_c_re"][:L]).transpose(4, 0, 1, 2, 3); cim = f(inputs["ssm_c_im"][:L]).transpose(4, 0, 1, 2, 3)
    bcs = np.stack([bre, bim, cre, cim], axis=3)
    shared["ssm_bc"] = np.ascontiguousarray(np.tile(bcs, (2, 1, 1, 1, 1, 1))).reshape(128, L * 2 * 4 * G * H)
    dd = f(inputs["ssm_d"][:L]).reshape(L, G, H).transpose(2, 0, 1)
    shared["dm"] = np.ascontiguousarray(np.tile(dd, (8, 1, 1))).reshape(128, L * G)
    maps = []
    for b in cores:
        m = dict(shared)
        xt = np.concatenate([_perm_tokens(x[b], NCL), _perm_tokens(ctx[b], NCC)], axis=0)
        m["xT"] = np.ascontiguousarray(xt.T)
        sc = np.stack([c[b].reshape(8, 128).T, c_ctx.reshape(8, 128).T], axis=2)
        m["scin"] = np.ascontiguousarray(sc).reshape(128, 16)
        maps.append(m)
    return maps


_NC_CACHE = {}


def kernel(**inputs):
    if 4 not in _NC_CACHE:
        _NC_CACHE[4] = build(4)
    nc = _NC_CACHE[4]
    maps = make_in_maps(inputs, 4, list(range(8)))
    res = run_bass_kernel_spmd(nc, maps, core_ids=list(range(8)))
    out = np.empty((8, TL, D), np.float32)
    for b in range(8):
        yt = np.asarray(res.results[b]["yT"], dtype=np.float32)
        out[b] = _unperm_tokens(np.ascontiguousarray(yt.T), NCL)
    return out
```

```python
import math
from contextlib import ExitStack
import numpy as np
import ml_dtypes
import concourse.bass as bass
import concourse.mybir as mybir
from concourse.bass_utils import run_bass_kernel_spmd

F32 = mybir.dt.float32
BF16 = mybir.dt.bfloat16
I32 = mybir.dt.int32
AF = mybir.ActivationFunctionType
ALU = mybir.AluOpType

D = 1024
KC = 8
DFF = 2816
FC = 22
TL = 4096
TCX = 256
TT = TL + TCX
G = 32
H = 16
NCL = 512
NCC = 32
NCH = NCL + NCC
ARENA_WORDS = 52600
ENGS = ['pe', 'act', 'dve', 'pool', 'sp']


class Prog:
    def __init__(self, nc):
        self.nc = nc
        self.q = {e: [] for e in ENGS}
        self.cnt = {}
        self.waited = {}
        self.lastw = {}
        self.readers = {}
        self.semnames = []

    def _sem(self, name):
        if name not in self.cnt:
            self.cnt[name] = 0
            self.semnames.append(name)

    def _wait(self, eng, tok):
        s, v = tok
        if self.waited.get((eng, s), 0) < v:
            self.q[eng].append(('w', s, v))
            self.waited[(eng, s)] = v

    def op(self, eng, fn, reads=(), writes=(), key=None, nosame=False):
        own = 'E_' + eng
        toks = []
        for r in reads:
            if r in self.lastw:
                toks.append(self.lastw[r])
        for w in writes:
            if w in self.lastw:
                toks.append(self.lastw[w])
            toks.extend(self.readers.get(w, ()))
        for t in toks:
            if eng == 'pe' and t[0] == 'E_pe':
                continue
            if nosame and t[0] == own:
                continue
            self._wait(eng, t)
        if key is None:
            self._sem(own)
            self.cnt[own] += 1
            tok = (own, self.cnt[own])
            self.q[eng].append(('o', fn, own, 1))
        else:
            self._sem(key)
            self.cnt[key] += 16
            tok = (key, self.cnt[key])
            self.q[eng].append(('o', fn, key, 16))
        for r in reads:
            self.readers.setdefault(r, []).append(tok)
        for w in writes:
            self.lastw[w] = tok
            self.readers[w] = []
        return tok

    def barrier(self):
        for e in ENGS:
            for s in self.semnames:
                if self.cnt[s] > 0:
                    self._wait(e, (s, self.cnt[s]))
        self.lastw = {}
        self.readers = {}

    def emit(self):
        nc = self.nc
        with ExitStack() as es:
            sems = {n: es.enter_context(nc.semaphore(n)) for n in self.semnames}
            block = es.enter_context(nc.Block())

            def mk(engname):
                def f(e):
                    for it in self.q[engname]:
                        if it[0] == 'w':
                            e.wait_ge(sems[it[1]], it[2])
                        else:
                            it[1](e).then_inc(sems[it[2]], it[3])
                return f
            block.tensor(mk('pe'))
            block.scalar(mk('act'))
            block.vector(mk('dve'))
            block.gpsimd(mk('pool'))
            block.sync(mk('sp'))


class Mem:
    def __init__(self, arena, nwords):
        self.a = arena
        self.n = nwords
        self.top = 0

    def f32(self, n, shape=None):
        nr = (n + 7) // 8 * 8
        off = self.top
        self.top += nr
        assert self.top <= self.n, ("SBUF overflow", self.top, self.n)
        return self.a[:, off:off + n]

    def bf(self, n):
        nw = ((n + 1) // 2 + 7) // 8 * 8
        off = self.top
        self.top += nw
        assert self.top <= self.n, ("SBUF overflow", self.top, self.n)
        return self.a[:, off:off + nw].bitcast(BF16)[:, 0:n]


def build(depth, stop=None):
    nc = bass.Bass("TRN2", target_bir_lowering=False)
    P = Prog(nc)
    L = depth

    def din(name, shape, dt=F32):
        return nc.dram_tensor(name, list(shape), dt, kind="ExternalInput").ap()

    xT = din("xT", [D, TT])
    scin = din("scin", [128, 16])
    wmod = din("w_mod", [L, D, 9 * D])
    bmodT = din("bmodT", [128, L * 72])
    ngT = din("ngT", [128, L * 24])
    fgT = din("fgT", [128, 8])
    w13d = [din("ffn1_w13", [L, D, 2 * DFF]), din("ffn2_w13", [L, D, 2 * DFF])]
    w2d = [din("ffn1_w2", [L, DFF, D]), din("ffn2_w2", [L, DFF, D])]
    wind = din("w_in", [L, D, 3 * D])
    wfod = din("w_fourier_out", [L, 512, D])
    wglud = din("w_glu", [L, 512, D])
    wsod = din("w_ssm_out", [L, 512, D])
    woutd = din("w_out", [L, D, D])
    ssml = din("ssm_small", [128, L * 2 * 3 * G])
    ssmbc = din("ssm_bc", [128, L * 2 * 4 * G * H])
    dmd = din("dm", [128, L * G])
    constd = din("consts", [128, 3 * 128 + 34])
    cscd = din("csc", [128, 256], BF16)
    tabd = din("tab", [8, 128, 32 * 2 * 512], BF16)
    tabcd = din("tabc", [128, 2 * 2 * 256], BF16)
    yT = nc.dram_tensor("yT", [D, TL], F32, kind="ExternalOutput").ap()
    hS = nc.dram_tensor("hS", [D, TT], F32).ap()
    upD = nc.dram_tensor("upD", [D, TT], BF16).ap()
    zD = nc.dram_tensor("zD", [512, TT], BF16).ap()
    ytD = nc.dram_tensor("ytD", [512, TT], BF16).ap()

    es = ExitStack()
    arena = es.enter_context(nc.sbuf_tensor("arena", [128, ARENA_WORDS], F32))
    psall = es.enter_context(nc.psum_tensor("psall", [128, 8 * 512], F32))
    M = Mem(arena, ARENA_WORDS)

    def bank(i, n=512):
        return psall[:, i * 512:i * 512 + n]

    mods = M.f32(L * 144)
    ngs = M.f32(L * 24)
    fgs = M.f32(8)
    cst = M.f32(3 * 128 + 34)
    ident = cst[:, 0:128]
    maskf = cst[:, 128:256]
    maskb = cst[:, 256:384]
    nvec = cst[:, 384:418]
    ssc = M.f32(16)
    epsT = M.f32(8)
    onesb = M.bf(128)
    persist_top = M.top

    def mod_ap(l, m, typ):
        v = mods[:, l * 144:(l + 1) * 144].rearrange("p (c t) -> p c t", t=2)
        return v[:, m * 8:(m + 1) * 8, typ]

    P.op('sp', lambda e: e.dma_start(out=ngs, in_=ngT), writes=['ngs'], key='p0')
    P.op('sp', lambda e: e.dma_start(out=fgs, in_=fgT), writes=['fgs'], key='p1')
    P.op('sp', lambda e: e.dma_start(out=cst, in_=constd), writes=['cst'], key='p2')
    P.op('sp', lambda e: e.dma_start(out=ssc, in_=scin), writes=['ssc'], key='p3')
    P.op('dve', lambda e: e.memset(onesb, 1.0), writes=['ones'])
    P.op('dve', lambda e: e.memset(epsT[:, 0:1], 1e-6), writes=['eps'])
    P.op('dve', lambda e: e.memset(epsT[:, 1:2], -math.pi), writes=['eps'])
    P.op('dve', lambda e: e.memset(epsT[:, 2:3], 0.0), writes=['eps'])
    P.op('act', lambda e: e.activation(out=ssc, in_=ssc, func=AF.Silu), reads=['ssc'], writes=['ssc'])
    eps_ap = epsT[:, 0:1]
    negpi_ap = epsT[:, 1:2]

    wm = [M.f32(8 * 512), M.f32(8 * 512)]
    bmt = M.f32(L * 72)
    P.op('sp', lambda e: e.dma_start(out=bmt, in_=bmodT), writes=['bmt'], key='p4')
    sscv = ssc.rearrange("p (k t) -> p k t", t=2)
    for l in range(L):
        psm = bank(l % 2, 144).rearrange("p (c t) -> p c t", t=2)
        wsrc = wmod[l].rearrange("(k p) n -> p k n", p=128)
        for blk in range(18):
            buf = wm[blk % 2]
            bufv = buf.rearrange("p (k n) -> p k n", k=8)
            P.op('sp', lambda e, bufv=bufv, wsrc=wsrc, blk=blk: e.dma_start(out=bufv, in_=wsrc[:, :, blk * 512:(blk + 1) * 512]),
                 writes=[('wm', blk % 2)], key='wm%d' % (blk % 2))
            for cc in range(4):
                ch = blk * 4 + cc
                for kc in range(8):
                    P.op('pe', lambda e, psm=psm, bufv=bufv, cc=cc, kc=kc, ch=ch: e.matmul(
                        out=psm[:, ch, :], lhsT=bufv[:, kc, cc * 128:(cc + 1) * 128], rhs=sscv[:, kc, :],
                        start=(kc == 0), stop=(kc == 7)),
                        reads=[('wm', blk % 2), 'ssc'], writes=[('psm', l % 2)])
        mv = mods[:, l * 144:(l + 1) * 144].rearrange("p (c t) -> p c t", t=2)
        for typ in range(2):
            P.op('dve', lambda e, mv=mv, psm=psm, typ=typ, l=l: e.tensor_tensor(
                out=mv[:, :, typ], in0=psm[:, :, typ], in1=bmt[:, l * 72:(l + 1) * 72], op=ALU.add),
                reads=[('psm', l % 2), 'bmt'], writes=['mods'])

    tiles512 = [(i * 512, 512, 0) for i in range(8)] + [(TL, 256, 1)]
    tiles256 = [(i * 256, 256, 0) for i in range(16)] + [(TL, 256, 1)]

    def sublayer_scalars(l, s, gate_scale):
        A = M.f32(16).rearrange("p (k t) -> p k t", t=2)
        SH = M.f32(16).rearrange("p (k t) -> p k t", t=2)
        GT = M.f32(16).rearrange("p (k t) -> p k t", t=2)
        gv = ngs[:, l * 24 + s * 8: l * 24 + s * 8 + 8]
        for typ in range(2):
            P.op('dve', lambda e, typ=typ: e.scalar_tensor_tensor(
                out=A[:, :, typ], in0=mod_ap(l, 3 * s + 1, typ), scalar=1.0, in1=gv, op0=ALU.add, op1=ALU.mult),
                reads=['mods', 'ngs'], writes=['A'])
            P.op('dve', lambda e, typ=typ: e.tensor_copy(out=SH[:, :, typ], in_=mod_ap(l, 3 * s, typ)),
                 reads=['mods'], writes=['SH'])
            P.op('dve', lambda e, typ=typ: e.tensor_scalar(
                out=GT[:, :, typ], in0=mod_ap(l, 3 * s + 2, typ), scalar1=float(gate_scale), scalar2=None, op0=ALU.mult),
                reads=['mods'], writes=['GT'])
        return A, SH, GT

    class NormBufs:
        def __init__(self, W, nx=1):
            self.W = W
            self.nx = nx
            self.hb = [M.f32(8 * W).rearrange("p (k w) -> p k w", k=8) for _ in range(2)]
            self.xns = [M.bf(8 * W).rearrange("p (k w) -> p k w", k=8) for _ in range(nx)]
            self.xn = self.xns[0]
            self.hsq = [M.bf(W) for _ in range(2)]
            self.rstd = M.f32(W)
            self.sq = M.f32(W)
            self.tmp = [M.f32(W) for _ in range(2)]

    def load_h(nb, i, src, t0, W):
        hb = nb.hb[i % 2]
        srcv = src.rearrange("(k p) t -> p k t", p=128)
        P.op('sp', lambda e: e.dma_start(out=hb[:, :, 0:W], in_=srcv[:, :, t0:t0 + W]),
             writes=[('hb', i % 2)], key='hb%d' % (i % 2))

    def rstd_of(nb, i, W, psbank):
        hb = nb.hb[i % 2]
        ssq = bank(psbank, W)
        for kc in range(8):
            hq = nb.hsq[kc % 2]
            P.op('dve', lambda e, hq=hq, kc=kc: e.tensor_tensor(out=hq[:, 0:W], in0=hb[:, kc, 0:W], in1=hb[:, kc, 0:W], op=ALU.mult),
                 reads=[('hb', i % 2)], writes=[('hsq', kc % 2)])
            P.op('pe', lambda e, hq=hq, kc=kc: e.matmul(out=ssq, lhsT=onesb, rhs=hq[:, 0:W], start=(kc == 0), stop=(kc == 7)),
                 reads=[('hsq', kc % 2), 'ones'], writes=[('ps', psbank)])
        P.op('act', lambda e: e.activation(out=nb.sq[:, 0:W], in_=ssq, func=AF.Sqrt, bias=eps_ap, scale=1.0 / D),
             reads=[('ps', psbank), 'eps'], writes=['sq'])
        P.op('dve', lambda e: e.reciprocal(out=nb.rstd[:, 0:W], in_=nb.sq[:, 0:W]), reads=['sq'], writes=['rstd'])

    def norm_tile(nb, i, W, typ, A, SH, psbank):
        hb = nb.hb[i % 2]
        xn_ = nb.xns[i % nb.nx]
        xres = ('xn', i % nb.nx)
        rstd_of(nb, i, W, psbank)
        for kc in range(8):
            tm = nb.tmp[kc % 2]
            P.op('dve', lambda e, tm=tm, kc=kc: e.tensor_tensor(out=tm[:, 0:W], in0=hb[:, kc, 0:W], in1=nb.rstd[:, 0:W], op=ALU.mult),
                 reads=[('hb', i % 2), 'rstd'], writes=[('tmp', kc % 2)])
            P.op('act', lambda e, tm=tm, kc=kc: e.activation(out=xn_[:, kc, 0:W], in_=tm[:, 0:W], func=AF.Identity,
                                                              bias=SH[:, kc, typ:typ + 1], scale=A[:, kc, typ:typ + 1]),
                 reads=[('tmp', kc % 2), 'A', 'SH'], writes=[xres])

    def load_w(dst3, src2, nk, key, res):
        srcv = src2.rearrange("(k p) n -> p k n", p=128)
        for k in range(nk):
            P.op('pool', lambda e, k=k: e.dma_start(out=dst3[:, k, :], in_=srcv[:, k, :], max_dma_last_dim=4096),
                 writes=[res], key=key)

    def ffn_phase(l, which, src, dst):
        P.barrier()
        M.top = persist_top
        s = 0 if which == 0 else 2
        w13 = M.bf(8 * 2 * DFF).rearrange("p (k f) -> p k f", k=8)
        w2 = M.bf(FC * D).rearrange("p (k d) -> p k d", k=FC)
        load_w(w13, w13d[which][l], 8, 'w13', 'w13')
        load_w(w2, w2d[which][l], FC, 'w2', 'w2')
        W = 256
        nb = NormBufs(W)
        gbuf = M.bf(FC * W).rearrange("p (k w) -> p k w", k=FC)
        sil = [M.f32(W) for _ in range(2)]
        A, SH, GT = sublayer_scalars(l, s, 0.5)
        tiles = tiles256
        n = len(tiles)

        def phaseA(i):
            for fc in range(FC):
                pa = bank(1 + (fc % 2) * 2, W)
                pb = bank(2 + (fc % 2) * 2, W)
                for kc in range(8):
                    P.op('pe', lambda e, pa=pa, fc=fc, kc=kc: e.matmul(out=pa, lhsT=w13[:, kc, fc * 128:(fc + 1) * 128], rhs=nb.xn[:, kc, 0:W],
                                                                          start=(kc == 0), stop=(kc == 7)),
                         reads=['w13', ('xn', 0)], writes=[('ps', 1 + (fc % 2) * 2)])
                for kc in range(8):
                    P.op('pe', lambda e, pb=pb, fc=fc, kc=kc: e.matmul(out=pb, lhsT=w13[:, kc, DFF + fc * 128:DFF + (fc + 1) * 128], rhs=nb.xn[:, kc, 0:W],
                                                                          start=(kc == 0), stop=(kc == 7)),
                         reads=['w13', ('xn', 0)], writes=[('ps', 2 + (fc % 2) * 2)])
                sl = sil[fc % 2]
                P.op('act', lambda e, pa=pa, sl=sl: e.activation(out=sl, in_=pa, func=AF.Silu),
                     reads=[('ps', 1 + (fc % 2) * 2)], writes=[('sil', fc % 2)])
                P.op('dve', lambda e, pb=pb, sl=sl, fc=fc: e.tensor_tensor(out=gbuf[:, fc, :], in0=sl, in1=pb, op=ALU.mult),
                     reads=[('sil', fc % 2), ('ps', 2 + (fc % 2) * 2)], writes=[('g', fc)])

        def phaseB(i):
            t0, _, typ = tiles[i]
            hb = nb.hb[i % 2]
            for dc in range(8):
                pb_i = 5 + dc % 3
                po = bank(pb_i, W)
                for fc in range(FC):
                    P.op('pe', lambda e, po=po, fc=fc, dc=dc: e.matmul(out=po, lhsT=w2[:, fc, dc * 128:(dc + 1) * 128], rhs=gbuf[:, fc, :],
                                                                          start=(fc == 0), stop=(fc == FC - 1)),
                         reads=['w2', ('g', fc)], writes=[('ps', pb_i)])
                P.op('dve', lambda e, po=po, dc=dc, typ=typ: e.scalar_tensor_tensor(
                    out=hb[:, dc, :], in0=po, scalar=GT[:, dc, typ:typ + 1], in1=hb[:, dc, :], op0=ALU.mult, op1=ALU.add),
                    reads=[('ps', pb_i), 'GT', ('hb', i % 2)], writes=[('hb', i % 2)])
            dstv = dst.rearrange("(k p) t -> p k t", p=128)
            P.op('pool', lambda e: e.dma_start(out=dstv[:, :, t0:t0 + W], in_=hb), reads=[('hb', i % 2)], key='st%d' % (i % 2))

        load_h(nb, 0, src, tiles[0][0], W)
        norm_tile(nb, 0, W, tiles[0][2], A, SH, 0)
        for i in range(n):
            if i + 1 < n:
                load_h(nb, i + 1, src, tiles[i + 1][0], W)
            phaseA(i)
            if i + 1 < n:
                norm_tile(nb, i + 1, W, tiles[i + 1][2], A, SH, 0)
            phaseB(i)

    def m1_phase(l):
        P.barrier()
        M.top = persist_top
        win = M.bf(8 * 1024).rearrange("p (k n) -> p k n", k=8)
        srcv = wind[l].rearrange("(k p) n -> p k n", p=128)
        for k in range(8):
            P.op('pool', lambda e, k=k: e.dma_start(out=win[:, k, :], in_=srcv[:, k, 0:1024], max_dma_last_dim=4096),
                 writes=['win'], key='win')
        W = 512
        nb = NormBufs(W, nx=2)
        stg = [M.bf(8 * W).rearrange("p (k w) -> p k w", k=8) for _ in range(2)]
        A, SH, GT = sublayer_scalars(l, 1, 1.0)
        tiles = tiles512
        n = len(tiles)
        upv = upD.rearrange("(k p) t -> p k t", p=128)
        load_h(nb, 0, hS, tiles[0][0], tiles[0][1])
        norm_tile(nb, 0, tiles[0][1], tiles[0][2], A, SH, 0)
        for i in range(n):
            t0, Wt, typ = tiles[i]
            if i + 1 < n:
                load_h(nb, i + 1, hS, tiles[i + 1][0], tiles[i + 1][1])
                norm_tile(nb, i + 1, tiles[i + 1][1], tiles[i + 1][2], A, SH, 0)
            st = stg[i % 2]
            xn_ = nb.xns[i % 2]
            for oc in range(8):
                pb_i = 1 + oc % 4
                po = bank(pb_i, Wt)
                for kc in range(8):
                    P.op('pe', lambda e, po=po, oc=oc, kc=kc, Wt=Wt, xn_=xn_: e.matmul(out=po, lhsT=win[:, kc, oc * 128:(oc + 1) * 128], rhs=xn_[:, kc, 0:Wt],
                                                                                 start=(kc == 0), stop=(kc == 7)),
                         reads=['win', ('xn', i % 2)], writes=[('ps', pb_i)])
                if oc % 2 == 0:
                    P.op('act', lambda e, po=po, oc=oc, Wt=Wt, st=st: e.copy(out=st[:, oc, 0:Wt], in_=po),
                         reads=[('ps', pb_i)], writes=[('stg', i % 2)])
                else:
                    P.op('dve', lambda e, po=po, oc=oc, Wt=Wt, st=st: e.tensor_copy(out=st[:, oc, 0:Wt], in_=po),
                         reads=[('ps', pb_i)], writes=[('stg', i % 2)])
            P.op('pool', lambda e, st=st, t0=t0, Wt=Wt: e.dma_start(out=upv[:, :, t0:t0 + Wt], in_=st[:, :, 0:Wt]),
                 reads=[('stg', i % 2)], writes=['upD'], key='stu%d' % (i % 2))

    def ssm_phase(l):
        P.barrier()
        M.top = persist_top
        BM = [[M.bf(G * 128).rearrange("p (g n) -> p g n", g=G) for _ in range(2)] for _ in range(2)]
        RM = [[M.bf(G * 128).rearrange("p (g n) -> p g n", g=G) for _ in range(2)] for _ in range(2)]
        MMb = M.bf(G * 128).rearrange("p (g n) -> p g n", g=G)
        LAMre = [M.f32(16) for _ in range(2)]
        LAMim = [M.f32(16) for _ in range(2)]
        LAMimN = [M.f32(16) for _ in range(2)]
        LAMimS = [M.f32(32).rearrange("p (r g) -> p r g", r=2) for _ in range(2)]
        dms = M.f32(G)
        main_top = M.top
        small = M.f32(2 * 3 * G).rearrange("p (d k g) -> p d k g", d=2, k=3)
        bc = M.f32(2 * 4 * G * H).rearrange("p (d k g h) -> p d k g h", d=2, k=4, g=G)
        P.op('sp', lambda e: e.dma_start(out=small, in_=ssml[:, l * 192:(l + 1) * 192].rearrange("p (d k g) -> p d k g", d=2, k=3)),
             writes=['small'], key='p0')
        P.op('sp', lambda e: e.dma_start(out=bc, in_=ssmbc[:, l * 4096:(l + 1) * 4096].rearrange("p (d k g h) -> p d k g h", d=2, k=4, g=G)),
             writes=['bc'], key='p1')
        P.op('sp', lambda e: e.dma_start(out=dms, in_=dmd[:, l * G:(l + 1) * G]), writes=['dms'], key='p2')
        NV = 34
        dt_ = M.f32(G)
        a_ = M.f32(G)
        th_ = M.f32(G)
        den = M.f32(G)
        t32a = M.f32(G)
        t32b = M.f32(G)
        kre = M.f32(G)
        kim = M.f32(G)
        PWre = M.f32(G * NV).rearrange("p (g n) -> p g n", g=G)
        PWim = M.f32(G * NV).rearrange("p (g n) -> p g n", g=G)
        bbre = M.f32(G * H).rearrange("p (g h) -> p g h", g=G)
        bbim = M.f32(G * H).rearrange("p (g h) -> p g h", g=G)
        t512a = M.f32(G * H).rearrange("p (g h) -> p g h", g=G)
        t512b = M.f32(G * H).rearrange("p (g h) -> p g h", g=G)
        NE = G * 8 * H

        def bigraw():
            return M.f32(NE)

        def v4(r):
            return r.rearrange("p (g j h) -> p g j h", g=G, j=8)

        def tabv(r, k):
            return r[:, k * G * NV:(k + 1) * G * NV].rearrange("p (g n) -> p g n", g=G)
        PRr, PIr, T1r, RZr, MMr = bigraw(), bigraw(), bigraw(), bigraw(), bigraw()
        PR, PI, T1, RZ, MMacc = v4(PRr), v4(PIr), v4(T1r), v4(RZr), v4(MMr)
        AN, YS, FR = tabv(T1r, 0), tabv(T1r, 1), tabv(T1r, 2)
        YI = RZr[:, 0:G * NV].bitcast(I32).rearrange("p (g n) -> p g n", g=G)
        MK, MG = tabv(RZr, 1), tabv(RZr, 2)
        SI, CO = tabv(PIr, 0), tabv(PIr, 1)
        mtmp = M.f32(128)
        for d in range(2):
            for ri in range(2):
                P.op('pool', lambda e, d=d, ri=ri: e.memset(BM[d][ri], 0.0), writes=[('BM', d)])
                P.op('pool', lambda e, d=d, ri=ri: e.memset(RM[d][ri], 0.0), writes=[('RM', d)])

        def V(fn, reads, writes, eng='dve'):
            P.op(eng, fn, reads=reads, writes=writes)

        nvb = nvec.unsqueeze(1).to_broadcast([128, G, NV])

        def cprod(slc, Xre, Xim, rd):
            pre = PWre[:, :, slc].unsqueeze(3).to_broadcast([128, G, 8, H])
            pim = PWim[:, :, slc].unsqueeze(3).to_broadcast([128, G, 8, H])
            xre = Xre.unsqueeze(2).to_broadcast([128, G, 8, H])
            xim = Xim.unsqueeze(2).to_broadcast([128, G, 8, H])
            V(lambda e: e.tensor_tensor(out=PR, in0=pre, in1=xre, op=ALU.mult), ['PW'] + rd, ['PR'])
            V(lambda e: e.tensor_tensor(out=T1, in0=pim, in1=xim, op=ALU.mult), ['PW'] + rd, ['T1'])
            V(lambda e: e.tensor_tensor(out=PR, in0=PR, in1=T1, op=ALU.subtract), ['PR', 'T1'], ['PR'])
            V(lambda e: e.tensor_tensor(out=PI, in0=pre, in1=xim, op=ALU.mult), ['PW'] + rd, ['PI'])
            V(lambda e: e.tensor_tensor(out=T1, in0=pim, in1=xre, op=ALU.mult), ['PW', 'PR'] + rd, ['T1'])
            V(lambda e: e.tensor_tensor(out=PI, in0=PI, in1=T1, op=ALU.add), ['PI', 'T1'], ['PI'])

        for d in range(2):
            lamre = small[:, d, 0, :]
            lamim = small[:, d, 1, :]
            lstep = small[:, d, 2, :]
            V(lambda e, lstep=lstep: e.activation(out=dt_, in_=lstep, func=AF.Exp), ['small'], ['dt'], 'act')
            V(lambda e, lamre=lamre: e.tensor_tensor(out=a_, in0=lamre, in1=dt_, op=ALU.mult), ['small', 'dt'], ['a'])
            V(lambda e, lamim=lamim: e.tensor_tensor(out=th_, in0=lamim, in1=dt_, op=ALU.mult), ['small', 'dt'], ['th'])
            V(lambda e, lamre=lamre: e.tensor_tensor(out=den, in0=lamre, in1=lamre, op=ALU.mult), ['small'], ['den'])
            V(lambda e, lamim=lamim: e.tensor_tensor(out=t32a, in0=lamim, in1=lamim, op=ALU.mult), ['small'], ['t32a'])
            V(lambda e: e.tensor_tensor(out=den, in0=den, in1=t32a, op=ALU.add), ['den', 't32a'], ['den'])
            V(lambda e: e.reciprocal(out=den, in_=den), ['den'], ['den'])
            P.barrier()
            thb = th_.unsqueeze(2).to_broadcast([128, G, NV])
            ab = a_.unsqueeze(2).to_broadcast([128, G, NV])
            V(lambda e, thb=thb: e.tensor_tensor(out=AN, in0=thb, in1=nvb, op=ALU.mult), ['th', 'cst'], ['AN'])
            V(lambda e, ab=ab: e.tensor_tensor(out=MG, in0=ab, in1=nvb, op=ALU.mult), ['a', 'cst'], ['MG'])
            V(lambda e: e.activation(out=MG, in_=MG, func=AF.Exp), ['MG'], ['MG'], 'act')
            for (dst_t, off) in ((SI, 64.5), (CO, 64.75)):
                V(lambda e, off=off: e.tensor_scalar(out=YS, in0=AN, scalar1=1.0 / (2 * math.pi), scalar2=off, op0=ALU.mult, op1=ALU.add),
                  ['AN', 'FR'], ['YS'])
                V(lambda e: e.tensor_copy(out=YI, in_=YS), ['YS'], ['YI'])
                V(lambda e: e.tensor_copy(out=FR, in_=YI), ['YI'], ['FR'])
                V(lambda e: e.tensor_tensor(out=FR, in0=YS, in1=FR, op=ALU.subtract), ['YS', 'FR'], ['FR'])
                V(lambda e: e.tensor_scalar(out=MK, in0=FR, scalar1=0.0, scalar2=None, op0=ALU.is_lt), ['FR'], ['MK'])
                V(lambda e: e.tensor_tensor(out=FR, in0=FR, in1=MK, op=ALU.add), ['FR', 'MK'], ['FR'])
                V(lambda e, dst_t=dst_t: e.activation(out=dst_t, in_=FR, func=AF.Sin, bias=negpi_ap, scale=2 * math.pi),
                  ['FR', 'eps'], ['SICO'], 'act')
            V(lambda e: e.tensor_tensor(out=PWre, in0=MG, in1=CO, op=ALU.mult), ['MG', 'SICO'], ['PW'])
            V(lambda e: e.tensor_tensor(out=PWim, in0=MG, in1=SI, op=ALU.mult), ['MG', 'SICO'], ['PW'])
            P.barrier()
            lbre = PWre[:, :, 9]
            lbim = PWim[:, :, 9]
            V(lambda e, lbre=lbre: e.tensor_scalar(out=t32a, in0=lbre, scalar1=-1.0, scalar2=None, op0=ALU.add), ['PW'], ['t32a'])
            V(lambda e, lamre=lamre: e.tensor_tensor(out=kre, in0=t32a, in1=lamre, op=ALU.mult), ['t32a', 'small'], ['kre'])
            V(lambda e, lbim=lbim, lamim=lamim: e.tensor_tensor(out=t32b, in0=lbim, in1=lamim, op=ALU.mult), ['PW', 'small'], ['t32b'])
            V(lambda e: e.tensor_tensor(out=kre, in0=kre, in1=t32b, op=ALU.add), ['kre', 't32b'], ['kre'])
            V(lambda e: e.tensor_tensor(out=kre, in0=kre, in1=den, op=ALU.mult), ['kre', 'den'], ['kre'])
            V(lambda e, lbim=lbim, lamre=lamre: e.tensor_tensor(out=kim, in0=lbim, in1=lamre, op=ALU.mult), ['PW', 'small'], ['kim'])
            V(lambda e, lamim=lamim: e.tensor_tensor(out=t32b, in0=t32a, in1=lamim, op=ALU.mult), ['t32a', 'small'], ['t32b'])
            V(lambda e: e.tensor_tensor(out=kim, in0=kim, in1=t32b, op=ALU.subtract), ['kim', 't32b'], ['kim'])
            V(lambda e: e.tensor_tensor(out=kim, in0=kim, in1=den, op=ALU.mult), ['kim', 'den'], ['kim'])
            bre = bc[:, d, 0, :, :]
            bim = bc[:, d, 1, :, :]
            cre = bc[:, d, 2, :, :]
            cim = bc[:, d, 3, :, :]
            kreb = kre.unsqueeze(2).to_broadcast([128, G, H])
            kimb = kim.unsqueeze(2).to_broadcast([128, G, H])
            V(lambda e, bre=bre, kreb=kreb: e.tensor_tensor(out=t512a, in0=bre, in1=kreb, op=ALU.mult), ['bc', 'kre'], ['t512a'])
            V(lambda e, bim=bim, kimb=kimb: e.tensor_tensor(out=t512b, in0=bim, in1=kimb, op=ALU.mult), ['bc', 'kim'], ['t512b'])
            V(lambda e: e.tensor_tensor(out=bbre, in0=t512a, in1=t512b, op=ALU.subtract), ['t512a', 't512b'], ['bb'])
            V(lambda e, bim=bim, kreb=kreb: e.tensor_tensor(out=t512a, in0=bim, in1=kreb, op=ALU.mult), ['bc', 'kre', 'bb'], ['t512a'])
            V(lambda e, bre=bre, kimb=kimb: e.tensor_tensor(out=t512b, in0=bre, in1=kimb, op=ALU.mult), ['bc', 'kim', 'bb'], ['t512b'])
            V(lambda e: e.tensor_tensor(out=bbim, in0=t512a, in1=t512b, op=ALU.add), ['t512a', 't512b'], ['bb'])
            for (dstl, srcw, sgn) in ((LAMre[d], PWre, 1.0), (LAMim[d], PWim, 1.0), (LAMimN[d], PWim, -1.0)):
                sv = srcw[:, :, 16].rearrange("p (gp g2) -> p gp g2", g2=2)
                V(lambda e, dstl=dstl, sv=sv, sgn=sgn: e.tensor_scalar(out=dstl[0:64, :], in0=sv[0:64, :, 0], scalar1=sgn, scalar2=None, op0=ALU.mult),
                  ['PW'], [('LAM', d)])
                V(lambda e, dstl=dstl, sv=sv, sgn=sgn: e.tensor_scalar(out=dstl[64:128, :], in0=sv[64:128, :, 1], scalar1=sgn, scalar2=None, op0=ALU.mult),
                  ['PW'], [('LAM', d)])
            V(lambda e, d=d: e.tensor_copy(out=LAMimS[d][:, 0, :], in_=LAMimN[d]), [('LAM', d)], [('LAM', d)])
            V(lambda e, d=d: e.tensor_copy(out=LAMimS[d][:, 1, :], in_=LAMim[d]), [('LAM', d)], [('LAM', d)])
            if d == 0:
                sB, sR, sZ = slice(17 + 1, 17 + 9), slice(9, 17), slice(17 + 9, 17 + 17)
            else:
                sB, sR, sZ = slice(8, 16), slice(17 + 0, 17 + 8), slice(0, 8)
            cprod(sB, bbre, bbim, ['bb'])
            V(lambda e: e.tensor_copy(out=PR[64:128], in_=PI[64:128]), ['PI', 'PR'], ['PR'])
            for g in range(G):
                pt = bank(g % 2, 128)
                P.op('pe', lambda e, pt=pt, g=g: e.transpose(out=pt, in_=PR[:, g, :, :].rearrange("p j h -> p (j h)"), identity=ident),
                     reads=['PR', 'cst'], writes=[('ps', g % 2)])
                c0 = (g % 2) * 64
                P.op('act', lambda e, pt=pt, g=g, c0=c0, d=d: e.copy(out=BM[d][0][:, g, c0:c0 + 64], in_=pt[:, 0:64]),
                     reads=[('ps', g % 2)], writes=[('BM', d)])
                P.op('dve', lambda e, pt=pt, g=g, c0=c0, d=d: e.tensor_copy(out=BM[d][1][:, g, c0:c0 + 64], in_=pt[:, 64:128]),
                     reads=[('ps', g % 2)], writes=[('BM', d)])
            cprod(sR, cre, cim, ['bc'])
            prv = PR.rearrange("p (gp g2) j h -> p gp g2 (j h)", g2=2)
            piv = PI.rearrange("p (gp g2) j h -> p gp g2 (j h)", g2=2)
            rre = RM[d][0].rearrange("p (gp g2) n -> p gp g2 n", g2=2)
            rim = RM[d][1].rearrange("p (gp g2) n -> p gp g2 n", g2=2)
            for hf in range(2):
                ps_ = slice(hf * 64, hf * 64 + 64)
                V(lambda e, ps_=ps_, hf=hf, rre=rre: e.tensor_copy(out=rre[ps_, :, hf, :], in_=prv[ps_, :, hf, :]), ['PR'], [('RM', d)])
                V(lambda e, ps_=ps_, hf=hf, rim=rim: e.tensor_scalar(out=rim[ps_, :, hf, :], in0=piv[ps_, :, hf, :], scalar1=-1.0, scalar2=None, op0=ALU.mult),
                  ['PI'], [('RM', d)])
            V(lambda e: e.tensor_copy(out=RZ[0:64], in_=PR[0:64]), ['PR', ('psM', 0), ('psM', 1)], ['RZ'])
            V(lambda e: e.tensor_scalar(out=RZ[64:128], in0=PI[64:128], scalar1=-1.0, scalar2=None, op0=ALU.mult), ['PI'], ['RZ'])
            cprod(sZ, bbre, bbim, ['bb'])
            V(lambda e: e.tensor_copy(out=PR[64:128], in_=PI[64:128]), ['PI', 'PR'], ['PR'])
            mask = maskf if d == 0 else maskb
            for g in range(G):
                pm = bank(2 + g % 2, 128)
                P.op('pe', lambda e, pm=pm, g=g: e.matmul(out=pm, lhsT=PR[:, g, :, :].rearrange("p j h -> p (j h)"),
                                                            rhs=RZ[:, g, :, :].rearrange("p j h -> p (j h)"), start=True, stop=True),
                     reads=['PR', 'RZ'], writes=[('psM', g % 2)])
                mg = MMacc[:, g, :, :].rearrange("p j h -> p (j h)")
                if d == 0:
                    V(lambda e, pm=pm, mg=mg, mask=mask: e.tensor_tensor(out=mg, in0=pm, in1=mask, op=ALU.mult), [('psM', g % 2), 'cst'], ['MMacc'])
                else:
                    V(lambda e, pm=pm, mask=mask: e.tensor_tensor(out=mtmp, in0=pm, in1=mask, op=ALU.mult), [('psM', g % 2), 'cst'], ['mtmp'])
                    V(lambda e, mg=mg: e.tensor_tensor(out=mg, in0=mg, in1=mtmp, op=ALU.add), ['mtmp', 'MMacc'], ['MMacc'])
                    V(lambda e, mg=mg, g=g: e.scalar_tensor_tensor(out=MMb[:, g, :], in0=ident, scalar=dms[:, g:g + 1], in1=mg, op0=ALU.mult, op1=ALU.add),
                      ['MMacc', 'dms', 'cst'], ['MMb'])
        P.barrier()
        M.top = main_top
        U = M.bf(G * NCH).rearrange("p (g c) -> p g c", g=G)
        SP = [[M.bf(16 * NCH).rearrange("p (g c) -> p g c", g=16) for _ in range(2)] for _ in range(2)]
        WB = [[M.f32(2 * 16 * 32).rearrange("p (r g c) -> p r g c", r=2, g=16) for _ in range(2)] for _ in range(2)]
        TS1 = [M.f32(32).rearrange("p (r g) -> p r g", r=2) for _ in range(2)]
        TS2 = [M.f32(32).rearrange("p (r g) -> p r g", r=2) for _ in range(2)]
        for j in range(8):
            P.op('sp', lambda e, j=j: e.dma_start(out=U[16 * j:16 * j + 16, :, 0:NCL],
                                                  in_=upD[512:1024, j * NCL:(j + 1) * NCL].rearrange("(g h) c -> h g c", h=H)),
                 reads=['upD'], writes=[('U', g_) for g_ in range(G)], key='ldU')
            P.op('sp', lambda e, j=j: e.dma_start(out=U[16 * j:16 * j + 16, :, NCL:NCH],
                                                  in_=upD[512:1024, TL + j * NCC:TL + (j + 1) * NCC].rearrange("(g h) c -> h g c", h=H)),
                 reads=['upD'], writes=[('U', g_) for g_ in range(G)], key='ldU')
        P.op('pool', lambda e: e.memset(SP[0][0][:, :, NCL:NCL + 1], 0.0), writes=[('SP', 0)])
        P.op('pool', lambda e: e.memset(SP[0][1][:, :, NCL:NCL + 1], 0.0), writes=[('SP', 0)])
        P.op('pool', lambda e: e.memset(SP[1][0][:, :, NCH - 1:NCH], 0.0), writes=[('SP', 1)])
        P.op('pool', lambda e: e.memset(SP[1][1][:, :, NCH - 1:NCH], 0.0), writes=[('SP', 1)])

        blocks = {0: [(NCL, True)] + [(b * 32, False) for b in range(16)],
                  1: [(NCL, True)] + [(b * 32, False) for b in range(15, -1, -1)]}
        prev_ap = {0: None, 1: None}
        lre_b = [LAMre[d].unsqueeze(1).to_broadcast([128, 2, 16]) for d in range(2)]
        for k in range(17):
            wbs = {}
            for d in range(2):
                c0, isctx = blocks[d][k]
                wb = WB[d][k % 2]
                wbs[d] = wb
                for ri in range(2):
                    pw = bank(d * 2 + ri, 512).rearrange("p (g c) -> p g c", g=16)
                    for gp in range(16):
                        for g2 in range(2):
                            g = 2 * gp + g2
                            P.op('pe', lambda e, pw=pw, gp=gp, g=g, g2=g2, d=d, ri=ri, c0=c0: e.matmul(
                                out=pw[:, gp, :], lhsT=BM[d][ri][:, g, :], rhs=U[:, g, c0:c0 + 32], start=(g2 == 0), stop=(g2 == 1)),
                                reads=[('BM', d), ('U', g)], writes=[('psW', d, ri)])
                    P.op('act', lambda e, pw=pw, wb=wb, ri=ri: e.copy(out=wb[:, ri, :, :], in_=pw),
                         reads=[('psW', d, ri)], writes=[('WB', d, k % 2)])
            for st_ in range(32):
                ops = {0: [], 1: []}
                for d in range(2):
                    cc = st_ if d == 0 else 31 - st_
                    wb = wbs[d]
                    cur = wb[:, :, :, cc]
                    pv = prev_ap[d]
                    res = ('WB', d, k % 2)
                    if pv is not None:
                        pvap, pvres = pv
                        t1 = TS1[d]
                        t2 = TS2[d]
                        ops[d].append((lambda e, t1=t1, pvap=pvap, d=d: e.tensor_tensor(out=t1, in0=pvap, in1=lre_b[d], op=ALU.mult),
                                       [pvres, ('LAM', d)], [('TS1', d)]))
                        ops[d].append((lambda e, t2=t2, pvap=pvap, d=d: e.tensor_tensor(out=t2, in0=pvap[:, ::-1, :], in1=LAMimS[d], op=ALU.mult),
                                       [pvres, ('LAM', d)], [('TS2', d)]))
                        ops[d].append((lambda e, cur=cur, t1=t1: e.tensor_tensor(out=cur, in0=cur, in1=t1, op=ALU.add),
                                       [res, ('TS1', d)], [res]))
                        ops[d].append((lambda e, cur=cur, t2=t2: e.tensor_tensor(out=cur, in0=cur, in1=t2, op=ALU.add),
                                       [res, ('TS2', d)], [res]))
                    prev_ap[d] = (cur, res)
                n0, n1 = len(ops[0]), len(ops[1])
                for j in range(max(n0, n1)):
                    for d in range(2):
                        if j < len(ops[d]):
                            fn, rd, wr = ops[d][j]
                            P.op('dve', fn, reads=rd, writes=wr, nosame=(n0 == n1 and n0 > 0))
            for d in range(2):
                c0, isctx = blocks[d][k]
                wb = wbs[d]
                for ri in range(2):
                    sp_ = SP[d][ri]
                    if d == 0:
                        if isctx:
                            P.op('act', lambda e, sp_=sp_, wb=wb, ri=ri: e.copy(out=sp_[:, :, NCL + 1:NCH], in_=wb[:, ri, :, 0:31]),
                                 reads=[('WB', d, k % 2)], writes=[('SP', d)])
                            P.op('act', lambda e, sp_=sp_, wb=wb, ri=ri: e.copy(out=sp_[:, :, 0:1], in_=wb[:, ri, :, 31:32]),
                                 reads=[('WB', d, k % 2)], writes=[('SP', d)])
                        else:
                            nv_ = 32 if c0 + 33 <= NCL else 31
                            P.op('act', lambda e, sp_=sp_, wb=wb, ri=ri, c0=c0, nv_=nv_: e.copy(out=sp_[:, :, c0 + 1:c0 + 1 + nv_], in_=wb[:, ri, :, 0:nv_]),
                                 reads=[('WB', d, k % 2)], writes=[('SP', d)])
                    else:
                        if isctx:
                            P.op('act', lambda e, sp_=sp_, wb=wb, ri=ri: e.copy(out=sp_[:, :, NCL:NCH - 1], in_=wb[:, ri, :, 1:32]),
                                 reads=[('WB', d, k % 2)], writes=[('SP', d)])
                            P.op('act', lambda e, sp_=sp_, wb=wb, ri=ri: e.copy(out=sp_[:, :, NCL - 1:NCL], in_=wb[:, ri, :, 0:1]),
                                 reads=[('WB', d, k % 2)], writes=[('SP', d)])
                        else:
                            if c0 == 0:
                                P.op('act', lambda e, sp_=sp_, wb=wb, ri=ri: e.copy(out=sp_[:, :, 0:31], in_=wb[:, ri, :, 1:32]),
                                     reads=[('WB', d, k % 2)], writes=[('SP', d)])
                            else:
                                P.op('act', lambda e, sp_=sp_, wb=wb, ri=ri, c0=c0: e.copy(out=sp_[:, :, c0 - 1:c0 + 31], in_=wb[:, ri, :, 0:32]),
                                     reads=[('WB', d, k % 2)], writes=[('SP', d)])
        P.barrier()
        for g in range(G):
            gp = g // 2
            b0 = (g % 4) * 2
            py = psall[:, b0 * 512:b0 * 512 + NCH]
            for (lo, hi) in ((0, NCL), (NCL, NCH)):
                out_ap = psall[:, b0 * 512 + lo:b0 * 512 + hi]
                P.op('pe', lambda e, out_ap=out_ap, g=g, lo=lo, hi=hi: e.matmul(out=out_ap, lhsT=MMb[:, g, :], rhs=U[:, g, lo:hi], start=True, stop=False),
                     reads=['MMb', ('U', g)], writes=[('psY', g % 4)])
                idx = 0
                for d in range(2):
                    for ri in range(2):
                        idx += 1
                        P.op('pe', lambda e, out_ap=out_ap, g=g, gp=gp, lo=lo, hi=hi, d=d, ri=ri, idx=idx: e.matmul(
                            out=out_ap, lhsT=RM[d][ri][:, g, :], rhs=SP[d][ri][:, gp, lo:hi], start=False, stop=(idx == 4)),
                            reads=[('RM', d), ('SP', d)], writes=[('psY', g % 4)])
            P.op('act', lambda e, py=py, g=g: e.activation(out=U[:, g, :], in_=py, func=AF.Gelu_apprx_tanh),
                 reads=[('psY', g % 4)], writes=[('U', g)])
        for j in range(8):
            P.op('sp', lambda e, j=j: e.dma_start(out=zD[:, j * NCL:(j + 1) * NCL].rearrange("(g h) c -> h g c", h=H),
                                                  in_=U[16 * j:16 * j + 16, :, 0:NCL]),
                 reads=[('U', g_) for g_ in range(G)], writes=['zD'], key='stz')
            P.op('sp', lambda e, j=j: e.dma_start(out=zD[:, TL + j * NCC:TL + (j + 1) * NCC].rearrange("(g h) c -> h g c", h=H),
                                                  in_=U[16 * j:16 * j + 16, :, NCL:NCH]),
                 reads=[('U', g_) for g_ in range(G)], writes=['zD'], key='stz')

    def fm_phase(l, dst):
        P.barrier()
        M.top = persist_top
        YT = M.bf(4 * TT).rearrange("p (g t) -> p g t", g=4)
        yt_top = M.top
        AB = M.bf(34 * 4 * 256).rearrange("p (c g n) -> p c g n", c=34, g=4)
        csc = M.bf(256)
        P.op('sp', lambda e: e.dma_start(out=csc, in_=cscd), writes=['csc'], key='p0')
        uf = [M.bf(4 * 512).rearrange("p (g t) -> p g t", g=4) for _ in range(2)]
        tb = [M.bf(4 * 2 * 512).rearrange("p (c s n) -> p c s n", c=4, s=2) for _ in range(2)]
        tbc = M.bf(2 * 2 * 256).rearrange("p (c s n) -> p c s n", c=2, s=2)
        upv = upD[0:512, :].rearrange("(g p) t -> p g t", p=128)
        for i, (t0, Wt, typ) in enumerate(tiles512):
            u = uf[i % 2]
            P.op('sp', lambda e, u=u, t0=t0, Wt=Wt: e.dma_start(out=u[:, :, 0:Wt], in_=upv[:, :, t0:t0 + Wt]),
                 reads=['upD'], writes=[('uf', i % 2)], key='uf%d' % (i % 2))
            for sub in range(Wt // 128):
                tc = t0 // 128 + sub
                for gh in range(2):
                    pb_i = (sub * 2 + gh) % 8
                    pa = bank(pb_i).rearrange("p (g n) -> p g n", g=2)
                    for g2 in range(2):
                        g = gh * 2 + g2
                        P.op('pe', lambda e, pa=pa, g2=g2, g=g, u=u, sub=sub: e.matmul(out=pa[:, g2, :], lhsT=u[:, g, sub * 128:(sub + 1) * 128], rhs=csc,
                                                                                        start=True, stop=True),
                             reads=[('uf', i % 2), 'csc'], writes=[('ps', pb_i)])
                    eng = 'act' if gh == 0 else 'dve'
                    if eng == 'act':
                        P.op('act', lambda e, pa=pa, tc=tc, gh=gh: e.copy(out=AB[:, tc, gh * 2:gh * 2 + 2, :], in_=pa), reads=[('ps', pb_i)], writes=['AB'])
                    else:
                        P.op('dve', lambda e, pa=pa, tc=tc, gh=gh: e.tensor_copy(out=AB[:, tc, gh * 2:gh * 2 + 2, :], in_=pa), reads=[('ps', pb_i)], writes=['AB'])
        cnt = 0
        for tt in range(8):
            for tcg in range(8):
                tbuf = tb[cnt % 2]
                P.op('sp', lambda e, tbuf=tbuf, tt=tt, tcg=tcg: e.dma_start(
                    out=tbuf, in_=tabd[tt, :, tcg * 4096:(tcg + 1) * 4096].rearrange("p (c s n) -> p c s n", c=4, s=2)),
                    writes=[('tb', cnt % 2)], key='tb%d' % (cnt % 2))
                for c4 in range(4):
                    tc = tcg * 4 + c4
                    for cs in range(2):
                        for g in range(4):
                            pb_i = (tt % 2) * 4 + g
                            first = (tcg == 0 and c4 == 0 and cs == 0)
                            last = (tcg == 7 and c4 == 3 and cs == 1)
                            P.op('pe', lambda e, pb_i=pb_i, tc=tc, g=g, cs=cs, tbuf=tbuf, c4=c4, first=first, last=last: e.matmul(
                                out=bank(pb_i), lhsT=AB[:, tc, g, cs * 128:(cs + 1) * 128], rhs=tbuf[:, c4, cs, :], start=first, stop=last),
                                reads=['AB', ('tb', cnt % 2)], writes=[('ps', pb_i)])
                cnt += 1
            for g in range(4):
                pb_i = (tt % 2) * 4 + g
                if g % 2 == 0:
                    P.op('act', lambda e, pb_i=pb_i, g=g, tt=tt: e.copy(out=YT[:, g, tt * 512:(tt + 1) * 512], in_=bank(pb_i)), reads=[('ps', pb_i)], writes=['YT'])
                else:
                    P.op('dve', lambda e, pb_i=pb_i, g=g, tt=tt: e.tensor_copy(out=YT[:, g, tt * 512:(tt + 1) * 512], in_=bank(pb_i)), reads=[('ps', pb_i)], writes=['YT'])
        P.op('sp', lambda e: e.dma_start(out=tbc, in_=tabcd.rearrange("p (c s n) -> p c s n", c=2, s=2)), writes=['tbc'], key='p1')
        for g in range(4):
            k = 0
            for c2 in range(2):
                for cs in range(2):
                    P.op('pe', lambda e, g=g, c2=c2, cs=cs, k=k: e.matmul(out=bank(g, 256), lhsT=AB[:, 32 + c2, g, cs * 128:(cs + 1) * 128], rhs=tbc[:, c2, cs, :],
                                                                          start=(k == 0), stop=(k == 3)),
                         reads=['AB', 'tbc'], writes=[('ps', g)])
                    k += 1
            P.op('act', lambda e, g=g: e.copy(out=YT[:, g, TL:TT], in_=bank(g, 256)), reads=[('ps', g)], writes=['YT'])

        P.op('sp', lambda e: e.dma_start(out=ytD.rearrange("(g p) t -> p g t", p=128), in_=YT), reads=['YT'], writes=['ytD'], key='p2')
        P.barrier()
        M.top = persist_top
        wg = M.bf(8 * 2048).rearrange("p (k n) -> p k n", k=8)
        srcv = wind[l].rearrange("(k p) n -> p k n", p=128)
        for k in range(8):
            P.op('pool', lambda e, k=k: e.dma_start(out=wg[:, k, :], in_=srcv[:, k, 1024:3072], max_dma_last_dim=4096), writes=['wg'], key='wg')
        wfo = M.bf(4 * D).rearrange("p (k n) -> p k n", k=4)
        wglu = M.bf(4 * D).rearrange("p (k n) -> p k n", k=4)
        wso = M.bf(4 * D).rearrange("p (k n) -> p k n", k=4)
        wout = M.bf(8 * D).rearrange("p (k n) -> p k n", k=8)
        load_w(wfo, wfod[l], 4, 'wfo', 'wfo')
        load_w(wglu, wglud[l], 4, 'wglu', 'wglu')
        load_w(wso, wsod[l], 4, 'wso', 'wso')
        load_w(wout, woutd[l], 8, 'wout', 'wout')
        W = 512
        nb = NormBufs(W, nx=2)
        zt = [M.bf(4 * W).rearrange("p (k w) -> p k w", k=4) for _ in range(2)]
        yb = [M.bf(4 * W).rearrange("p (k w) -> p k w", k=4) for _ in range(2)]
        ytv = ytD.rearrange("(g p) t -> p g t", p=128)
        glu = M.bf(4 * W).rearrange("p (k w) -> p k w", k=4)
        mb = M.bf(8 * W).rearrange("p (k w) -> p k w", k=8)
        sgb = M.bf(16 * W).rearrange("p (k w) -> p k w", k=16)
        sg = [M.f32(W) for _ in range(2)]
        e1 = [M.f32(W) for _ in range(2)]
        e2 = [M.f32(W) for _ in range(2)]
        A, SH, GT = sublayer_scalars(l, 1, 1.0)
        tiles = tiles512
        n = len(tiles)
        zv = zD.rearrange("(k p) t -> p k t", p=128)
        dstv = dst.rearrange("(k p) t -> p k t", p=128)
        load_h(nb, 0, hS, tiles[0][0], tiles[0][1])
        norm_tile(nb, 0, tiles[0][1], tiles[0][2], A, SH, 0)
        for i in range(n):
            t0, Wt, typ = tiles[i]
            hb = nb.hb[i % 2]
            xn_ = nb.xns[i % 2]
            xres = ('xn', i % 2)
            z = zt[i % 2]
            P.op('sp', lambda e, z=z, t0=t0, Wt=Wt: e.dma_start(out=z[:, :, 0:Wt], in_=zv[:, :, t0:t0 + Wt]), reads=['zD'], writes=[('zt', i % 2)], key='zt%d' % (i % 2))
            y_ = yb[i % 2]
            P.op('sp', lambda e, y_=y_, t0=t0, Wt=Wt: e.dma_start(out=y_[:, :, 0:Wt], in_=ytv[:, :, t0:t0 + Wt]), reads=['ytD'], writes=[('yb', i % 2)], key='yb%d' % (i % 2))
            for oc in range(16):
                pb_i = 1 + oc % 4
                for kc in range(8):
                    P.op('pe', lambda e, pb_i=pb_i, oc=oc, kc=kc, Wt=Wt, xn_=xn_: e.matmul(out=bank(pb_i, Wt), lhsT=wg[:, kc, oc * 128:(oc + 1) * 128], rhs=xn_[:, kc, 0:Wt],
                                                                                       start=(kc == 0), stop=(kc == 7)),
                         reads=['wg', xres], writes=[('ps', pb_i)])
                P.op('act', lambda e, pb_i=pb_i, oc=oc, Wt=Wt: e.activation(out=sgb[:, oc, 0:Wt], in_=bank(pb_i, Wt), func=AF.Sigmoid),
                     reads=[('ps', pb_i)], writes=[('sgb', oc)])
            if i + 1 < n:
                load_h(nb, i + 1, hS, tiles[i + 1][0], tiles[i + 1][1])
                norm_tile(nb, i + 1, tiles[i + 1][1], tiles[i + 1][2], A, SH, 0)
            for oc in range(4):
                pa_i, pg_i = 5 + (oc % 2) * 2 - (0 if oc % 2 == 0 else 0), 6 + (oc % 2) * 2 - (0 if oc % 2 == 0 else 8 * 0)
                if oc % 2 == 1:
                    pa_i, pg_i = 7, 1
                for (pb_i, off) in ((pa_i, 0), (pg_i, 512)):
                    for kc in range(4):
                        P.op('pe', lambda e, pb_i=pb_i, off=off, oc=oc, kc=kc, z=z, Wt=Wt: e.matmul(
                            out=bank(pb_i, Wt), lhsT=wglu[:, kc, off + oc * 128:off + (oc + 1) * 128], rhs=z[:, kc, 0:Wt], start=(kc == 0), stop=(kc == 3)),
                            reads=['wglu', ('zt', i % 2)], writes=[('ps', pb_i)])
                s_ = sg[oc % 2]
                P.op('act', lambda e, s_=s_, pg_i=pg_i, Wt=Wt: e.activation(out=s_[:, 0:Wt], in_=bank(pg_i, Wt), func=AF.Sigmoid), reads=[('ps', pg_i)], writes=[('sg', oc % 2)])
                P.op('dve', lambda e, s_=s_, pa_i=pa_i, oc=oc, Wt=Wt: e.tensor_tensor(out=glu[:, oc, 0:Wt], in0=s_[:, 0:Wt], in1=bank(pa_i, Wt), op=ALU.mult),
                     reads=[('sg', oc % 2), ('ps', pa_i)], writes=[('glu', oc)])
            sets = [(2, 3), (4, 5), (6, 7)]
            for dc in range(8):
                bf_i, bs_i = sets[dc % 3]
                par = dc % 2
                for kc in range(4):
                    P.op('pe', lambda e, bf_i=bf_i, dc=dc, kc=kc, Wt=Wt, y_=y_: e.matmul(out=bank(bf_i, Wt), lhsT=wfo[:, kc, dc * 128:(dc + 1) * 128], rhs=y_[:, kc, 0:Wt],
                                                                                       start=(kc == 0), stop=(kc == 3)),
                         reads=['wfo', ('yb', i % 2)], writes=[('ps', bf_i)])
                for kc in range(4):
                    P.op('pe', lambda e, bs_i=bs_i, dc=dc, kc=kc, Wt=Wt: e.matmul(out=bank(bs_i, Wt), lhsT=wso[:, kc, dc * 128:(dc + 1) * 128], rhs=glu[:, kc, 0:Wt],
                                                                                start=(kc == 0), stop=(kc == 3)),
                         reads=['wso', ('glu', kc)], writes=[('ps', bs_i)])
                x1, x2 = e1[par], e2[par]
                P.op('dve', lambda e, x1=x1, bf_i=bf_i, dc=dc, Wt=Wt: e.tensor_tensor(out=x1[:, 0:Wt], in0=sgb[:, dc, 0:Wt], in1=bank(bf_i, Wt), op=ALU.mult),
                     reads=[('sgb', dc), ('ps', bf_i)], writes=[('e1', par)])
                P.op('dve', lambda e, x2=x2, bs_i=bs_i, dc=dc, Wt=Wt: e.tensor_tensor(out=x2[:, 0:Wt], in0=sgb[:, 8 + dc, 0:Wt], in1=bank(bs_i, Wt), op=ALU.mult),
                     reads=[('sgb', 8 + dc), ('ps', bs_i)], writes=[('e2', par)])
                P.op('pool', lambda e, x1=x1, x2=x2, dc=dc, Wt=Wt: e.tensor_tensor(out=mb[:, dc, 0:Wt], in0=x1[:, 0:Wt], in1=x2[:, 0:Wt], op=ALU.add),
                     reads=[('e1', par), ('e2', par)], writes=[('mb', dc)])
            for dc in range(8):
                pb_i = 1 if dc % 2 == 0 else 2
                for kc in range(8):
                    P.op('pe', lambda e, pb_i=pb_i, dc=dc, kc=kc, Wt=Wt: e.matmul(out=bank(pb_i, Wt), lhsT=wout[:, kc, dc * 128:(dc + 1) * 128], rhs=mb[:, kc, 0:Wt],
                                                                                start=(kc == 0), stop=(kc == 7)),
                         reads=['wout', ('mb', kc)], writes=[('ps', pb_i)])
                P.op('dve', lambda e, pb_i=pb_i, dc=dc, typ=typ, Wt=Wt, hb=hb: e.scalar_tensor_tensor(
                    out=hb[:, dc, 0:Wt], in0=bank(pb_i, Wt), scalar=GT[:, dc, typ:typ + 1], in1=hb[:, dc, 0:Wt], op0=ALU.mult, op1=ALU.add),
                    reads=[('ps', pb_i), 'GT', ('hb', i % 2)], writes=[('hb', i % 2)])
            P.op('pool', lambda e, hb=hb, t0=t0, Wt=Wt: e.dma_start(out=dstv[:, :, t0:t0 + Wt], in_=hb[:, :, 0:Wt]), reads=[('hb', i % 2)], key='st%d' % (i % 2))

    def final_phase(src):
        P.barrier()
        M.top = persist_top
        W = 512
        nb = NormBufs(W)
        ob = [M.f32(8 * W).rearrange("p (k w) -> p k w", k=8) for _ in range(2)]
        yv = yT.rearrange("(k p) t -> p k t", p=128)
        tiles = tiles512[:8]
        load_h(nb, 0, src, 0, W)
        for i, (t0, Wt, typ) in enumerate(tiles):
            if i + 1 < len(tiles):
                load_h(nb, i + 1, src, tiles[i + 1][0], W)
            hb = nb.hb[i % 2]
            rstd_of(nb, i, W, 0)
            o = ob[i % 2]
            for kc in range(8):
                tm = nb.tmp[kc % 2]
                P.op('dve', lambda e, tm=tm, kc=kc, hb=hb: e.tensor_tensor(out=tm, in0=hb[:, kc, :], in1=nb.rstd, op=ALU.mult),
                     reads=[('hb', i % 2), 'rstd'], writes=[('tmp', kc % 2)])
                P.op('act', lambda e, tm=tm, kc=kc, o=o: e.activation(out=o[:, kc, :], in_=tm, func=AF.Identity, scale=fgs[:, kc:kc + 1]),
                     reads=[('tmp', kc % 2), 'fgs'], writes=[('ob', i % 2)])
            P.op('pool', lambda e, o=o, t0=t0: e.dma_start(out=yv[:, :, t0:t0 + W], in_=o), reads=[('ob', i % 2)], key='sty%d' % (i % 2))

    src = xT
    for l in range(L):
        ffn_phase(l, 0, src, hS)
        src = hS
        if stop == ('ffn1', l):
            break
        m1_phase(l)
        ssm_phase(l)
        fm_phase(l, hS)
        if stop == ('mix', l):
            break
        ffn_phase(l, 1, hS, hS)
    final_phase(hS)
    P.barrier()
    P.emit()
    es.close()
    return nc


def _perm_tokens(a, nchunk):
    T = a.shape[0]
    return a.reshape(nchunk, 8, -1).transpose(1, 0, 2).reshape(T, -1)


def _unperm_tokens(a, nchunk):
    T = a.shape[0]
    return a.reshape(8, nchunk, -1).transpose(1, 0, 2).reshape(T, -1)


_CONST_CACHE = {}


def _host_consts():
    if _CONST_CACHE:
        return _CONST_CACHE
    bf = ml_dtypes.bfloat16
    cst = np.zeros((128, 3 * 128 + 34), np.float32)
    cst[:, 0:128] = np.eye(128, dtype=np.float32)
    jj = np.arange(128) // 16
    cst[:, 128:256] = (jj[None, :] >= jj[:, None]).astype(np.float32)
    cst[:, 256:384] = (jj[:, None] >= jj[None, :]).astype(np.float32)
    nv = np.concatenate([np.arange(-8, 9), np.arange(8, -9, -1)]).astype(np.float32)
    cst[:, 384:418] = nv[None, :]
    c = np.arange(128)
    ang = 2 * np.pi * np.outer(c, c) / 128.0
    csc = np.concatenate([np.cos(ang), np.sin(ang)], axis=1) / np.sqrt(128.0)

    def seq_tab(T, nchunk):
        pos = np.arange(T)
        tok = 8 * (pos % nchunk) + pos // nchunk
        prod = (np.outer(tok, tok) % T).astype(np.float64)
        ang = 2 * np.pi * prod / T
        s = 1.0 / np.sqrt(T)
        return (np.cos(ang) * s).astype(np.float32), (-np.sin(ang) * s).astype(np.float32)
    Cl, Sl = seq_tab(TL, NCL)
    tab = np.stack([Cl, Sl], axis=0)
    tab = tab.reshape(2, 32, 128, 8, 512)
    tab = np.ascontiguousarray(tab.transpose(3, 2, 1, 0, 4)).reshape(8, 128, 32 * 2 * 512).astype(bf)
    Cc, Sc = seq_tab(TCX, NCC)
    tabc = np.stack([Cc, Sc], axis=0).reshape(2, 2, 128, 256)
    tabc = np.ascontiguousarray(tabc.transpose(2, 1, 0, 3)).reshape(128, 2 * 2 * 256).astype(bf)
    _CONST_CACHE.update(consts=cst, csc=csc.astype(bf), tab=tab, tabc=tabc)
    return _CONST_CACHE


def make_in_maps(inputs, depth, cores):
    f = lambda a: np.ascontiguousarray(np.asarray(a, dtype=np.float32))
    x = f(inputs["x"]); c = f(inputs["c"]); ctx = f(inputs["ctx"]); c_ctx = f(inputs["c_ctx"])
    L = depth
    consts = _host_consts()
    shared = dict(consts)
    shared["w_mod"] = f(inputs["w_mod"][:L])
    shared["bmodT"] = np.ascontiguousarray(f(inputs["b_mod"][:L]).reshape(L, 72, 128).transpose(2, 0, 1)).reshape(128, L * 72)
    shared["ngT"] = np.ascontiguousarray(f(inputs["norm_g"][:L]).reshape(L, 3, 8, 128).transpose(3, 0, 1, 2)).reshape(128, L * 24)
    shared["fgT"] = np.ascontiguousarray(f(inputs["final_g"]).reshape(8, 128).T)
    for k in ["ffn1_w13", "ffn1_w2", "ffn2_w13", "ffn2_w2", "w_in", "w_fourier_out", "w_glu", "w_ssm_out", "w_out"]:
        shared[k] = f(inputs[k][:L])
    lre = f(inputs["ssm_lambda_re"][:L]); lim = f(inputs["ssm_lambda_im"][:L]); ls = f(inputs["ssm_log_step"][:L])
    sm = np.stack([lre.transpose(3, 0, 1, 2), lim.transpose(3, 0, 1, 2),
                   np.broadcast_to(ls[None], (64, L, 2, G))], axis=3)
    shared["ssm_small"] = np.ascontiguousarray(np.tile(sm, (2, 1, 1, 1, 1))).reshape(128, L * 2 * 3 * G)
    bre = f(inputs["ssm_b_re"][:L]).transpose(3, 0, 1, 2, 4); bim = f(inputs["ssm_b_im"][:L]).transpose(3, 0, 1, 2, 4)
    cre = f(inputs["ssm_c_re"][:L]).transpose(4, 0, 1, 2, 3); cim = f(inputs["ssm_c_im"][:L]).transpose(4, 0, 1, 2, 3)
    bcs = np.stack([bre, bim, cre, cim], axis=3)
    shared["ssm_bc"] = np.ascontiguousarray(np.tile(bcs, (2, 1, 1, 1, 1, 1))).reshape(128, L * 2 * 4 * G * H)
    dd = f(inputs["ssm_d"][:L]).reshape(L, G, H).transpose(2, 0, 1)
    shared["dm"] = np.ascontiguousarray(np.tile(dd, (8, 1, 1))).reshape(128, L * G)
    maps = []
    for b in cores:
        m = dict(shared)
        xt = np.concatenate([_perm_tokens(x[b], NCL), _perm_tokens(ctx[b], NCC)], axis=0)
        m["xT"] = np.ascontiguousarray(xt.T)
        sc = np.stack([c[b].reshape(8, 128).T, c_ctx.reshape(8, 128).T], axis=2)
        m["scin"] = np.ascontiguousarray(sc).reshape(128, 16)
        maps.append(m)
    return maps


_NC_CACHE = {}


def kernel(**inputs):
    if 4 not in _NC_CACHE:
        _NC_CACHE[4] = build(4)
    nc = _NC_CACHE[4]
    maps = make_in_maps(inputs, 4, list(range(8)))
    res = run_bass_kernel_spmd(nc, maps, core_ids=list(range(8)))
    out = np.empty((8, TL, D), np.float32)
    for b in range(8):
        yt = np.asarray(res.results[b]["yT"], dtype=np.float32)
        out[b] = _unperm_tokens(np.ascontiguousarray(yt.T), NCL)
    return out
```
